# Optimizing a Trainium2 kernel written in Bass

```python
import math
import jax, jax.numpy as jnp
from jax import lax
import numpy as np

D_MODEL = 1024
BATCH = 8
SEQ = 4096
DEPTH = 2

N_META = 16
Q_BLOCK = 128
SB_HEADS = 8
SB_HEAD_DIM = 64
SB_WIDTH = SB_HEADS * SB_HEAD_DIM
MLA_HEADS = 8
MLA_NOPE = 64
MLA_ROPE = 32
MLA_V = 64
Q_LORA = 384
KV_LORA = 256
MLA_WIDTH = MLA_HEADS * MLA_V
MIX_WIDTH = SB_WIDTH + MLA_WIDTH
IN_COLS = 3 * SB_WIDTH + Q_LORA + KV_LORA + MLA_ROPE
D_FF = -(-8 * D_MODEL // (3 * 256)) * 256
ROPE_THETA = 10000.0
EPS = 1e-6

kernel_name = "hybrid_stickbreaking_mla_block"


def rms_norm(x, gain):
    xf = x.astype(jnp.float32)
    y = xf * lax.rsqrt(jnp.mean(xf * xf, axis=-1, keepdims=True) + EPS)
    return y.astype(x.dtype) * gain.astype(x.dtype)


def rope_tables(length):
    inv_freq = 1.0 / (ROPE_THETA ** (jnp.arange(0, MLA_ROPE, 2, dtype=jnp.float32) / MLA_ROPE))
    ang = jnp.arange(length, dtype=jnp.float32)[:, None] * inv_freq[None, :]
    return jnp.cos(ang), jnp.sin(ang)


def apply_rope(x, cos, sin):
    xf = x.astype(jnp.float32)
    half = MLA_ROPE // 2
    x1, x2 = xf[..., :half], xf[..., half:]
    c = cos[None, :, None, :]
    s = sin[None, :, None, :]
    return jnp.concatenate([x1 * c - x2 * s, x2 * c + x1 * s], axis=-1).astype(x.dtype)


def stick_breaking_block(q, k, v, q_start):
    scale = 1.0 / math.sqrt(SB_HEAD_DIM)
    z = jnp.einsum('bqhd,bkhd->bhqk', q.astype(jnp.float32), k.astype(jnp.float32)) * scale
    t_idx = q_start + jnp.arange(q.shape[1])
    s_idx = jnp.arange(k.shape[1])
    causal = s_idx[None, :] < t_idx[:, None]
    log_1m = jnp.where(causal, jax.nn.log_sigmoid(-z), 0.0)
    suffix = lax.cumsum(log_1m, axis=3, reverse=True) - log_1m
    a = jnp.where(causal, jnp.exp(jax.nn.log_sigmoid(z) + suffix), 0.0)
    o = jnp.einsum('bhqk,bkhd->bqhd', a, v.astype(jnp.float32))
    return o.astype(v.dtype)


def mla_block(q, k, v, q_start):
    scale = 1.0 / math.sqrt(MLA_NOPE + MLA_ROPE)
    s = jnp.einsum('bqhd,bkhd->bhqk', q.astype(jnp.float32), k.astype(jnp.float32)) * scale
    t_idx = q_start + jnp.arange(q.shape[1])
    s_idx = jnp.arange(k.shape[1])
    causal = s_idx[None, :] <= t_idx[:, None]
    s = jnp.where(causal, s, jnp.finfo(jnp.float32).min)
    p = jax.nn.softmax(s, axis=-1)
    o = jnp.einsum('bhqk,bkhd->bqhd', p, v.astype(jnp.float32))
    return o.astype(v.dtype)


def setup_inputs(seed: int = 0) -> dict:
    key = jax.random.key(seed)
    ks = jax.random.split(key, 20)
    f32 = jnp.float32

    def w(k, shape, fan_in):
        return jax.random.normal(k, shape, f32) * (fan_in ** -0.5)

    def gain(k, shape):
        return 1.0 + 0.05 * jax.random.normal(k, shape, f32)

    return {
        "x": jax.random.normal(ks[0], (BATCH, SEQ, D_MODEL), f32),
        "meta_tokens": jax.random.normal(ks[1], (N_META, D_MODEL), f32),
        "w_in": w(ks[2], (DEPTH, D_MODEL, IN_COLS), D_MODEL),
        "q_lat_norm": gain(ks[3], (DEPTH, Q_LORA)),
        "kv_lat_norm": gain(ks[4], (DEPTH, KV_LORA)),
        "w_uq": w(ks[5], (DEPTH, Q_LORA, MLA_HEADS * (MLA_NOPE + MLA_ROPE)), Q_LORA),
        "w_ukv": w(ks[6], (DEPTH, KV_LORA, MLA_HEADS * (MLA_NOPE + MLA_V)), KV_LORA),
        "sb_out_norm": gain(ks[7], (DEPTH, SB_WIDTH)),
        "mla_out_norm": gain(ks[8], (DEPTH, MLA_WIDTH)),
        "w_o": w(ks[9], (DEPTH, MIX_WIDTH, D_MODEL), MIX_WIDTH),
        "pre_mix_norm": gain(ks[10], (DEPTH, D_MODEL)),
        "post_mix_norm": gain(ks[11], (DEPTH, D_MODEL)),
        "pre_ffn_norm": gain(ks[12], (DEPTH, D_MODEL)),
        "post_ffn_norm": gain(ks[13], (DEPTH, D_MODEL)),
        "w_gate": w(ks[14], (DEPTH, D_MODEL, D_FF), D_MODEL),
        "w_up": w(ks[15], (DEPTH, D_MODEL, D_FF), D_MODEL),
        "w_down": w(ks[16], (DEPTH, D_FF, D_MODEL), D_FF),
    }


def reference(x, meta_tokens, w_in, q_lat_norm, kv_lat_norm, w_uq, w_ukv, sb_out_norm, mla_out_norm,
              w_o, pre_mix_norm, post_mix_norm, pre_ffn_norm, post_ffn_norm, w_gate, w_up, w_down):
    B = x.shape[0]
    n_real = x.shape[1]
    meta = jnp.broadcast_to(meta_tokens.astype(x.dtype)[None], (B, N_META, D_MODEL))
    h = jnp.concatenate([meta, x], axis=1)
    L = h.shape[1]
    cos, sin = rope_tables(L)
    blocks = [(0, N_META)] + [(N_META + i * Q_BLOCK, N_META + (i + 1) * Q_BLOCK)
                              for i in range(n_real // Q_BLOCK)]
    o1 = 3 * SB_WIDTH
    o2 = o1 + Q_LORA
    o3 = o2 + KV_LORA

    for layer in range(DEPTH):
        u = rms_norm(h, pre_mix_norm[layer])
        proj = jnp.einsum('bld,dc->blc', u, w_in[layer])
        q_sb = proj[..., 0:SB_WIDTH].reshape(B, L, SB_HEADS, SB_HEAD_DIM)
        k_sb = proj[..., SB_WIDTH:2 * SB_WIDTH].reshape(B, L, SB_HEADS, SB_HEAD_DIM)
        v_sb = proj[..., 2 * SB_WIDTH:o1].reshape(B, L, SB_HEADS, SB_HEAD_DIM)
        c_q = rms_norm(proj[..., o1:o2], q_lat_norm[layer])
        c_kv = rms_norm(proj[..., o2:o3], kv_lat_norm[layer])
        k_rope = proj[..., o3:].reshape(B, L, 1, MLA_ROPE)

        q_m = jnp.einsum('blr,rc->blc', c_q, w_uq[layer]).reshape(B, L, MLA_HEADS, MLA_NOPE + MLA_ROPE)
        kv_m = jnp.einsum('blr,rc->blc', c_kv, w_ukv[layer]).reshape(B, L, MLA_HEADS, MLA_NOPE + MLA_V)
        q_mla = jnp.concatenate([q_m[..., :MLA_NOPE], apply_rope(q_m[..., MLA_NOPE:], cos, sin)], axis=-1)
        k_rope_r = jnp.broadcast_to(apply_rope(k_rope, cos, sin), (B, L, MLA_HEADS, MLA_ROPE))
        k_mla = jnp.concatenate([kv_m[..., :MLA_NOPE], k_rope_r], axis=-1)
        v_mla = kv_m[..., MLA_NOPE:]

        o_sb = jnp.concatenate([stick_breaking_block(q_sb[:, s:e], k_sb[:, :e], v_sb[:, :e], s)
                                for (s, e) in blocks], axis=1)
        o_mla = jnp.concatenate([mla_block(q_mla[:, s:e], k_mla[:, :e], v_mla[:, :e], s)
                                 for (s, e) in blocks], axis=1)

        merged = jnp.concatenate([rms_norm(o_sb.reshape(B, L, SB_WIDTH), sb_out_norm[layer]),
                                  rms_norm(o_mla.reshape(B, L, MLA_WIDTH), mla_out_norm[layer])], axis=-1)
        mix = jnp.einsum('blc,cd->bld', merged, w_o[layer])
        h = h + rms_norm(mix, post_mix_norm[layer])

        f = rms_norm(h, pre_ffn_norm[layer])
        g = jnp.einsum('bld,df->blf', f, w_gate[layer])
        up = jnp.einsum('bld,df->blf', f, w_up[layer])
        ffn = jnp.einsum('blf,fd->bld', jax.nn.silu(g) * up, w_down[layer])
        h = h + rms_norm(ffn, post_ffn_norm[layer])

    return h[:, N_META:]
```

```python
from contextlib import ExitStack
import math

import numpy as np
import ml_dtypes
import concourse.bass as bass
import concourse.mybir as mybir
from concourse.bass_utils import run_bass_kernel_spmd

F32 = mybir.dt.float32
BF16 = mybir.dt.bfloat16
AF = mybir.ActivationFunctionType
ALU = mybir.AluOpType

D_MODEL = 1024
N_META = 16
SBW = 512
Q_LORA = 384
KV_LORA = 256
ROPE = 32
NOPE = 64
D_FF = 2816
EPS = 1e-6
NEG = -30000.0
SCALE_MLA = 1.0 / math.sqrt(96.0)


class Buf:
    __slots__ = ("name", "t", "w", "r", "dkey")

    def __init__(self, name, t):
        self.name = name
        self.t = t
        self.w = {}
        self.r = {}
        self.dkey = None


class RR:
    def __init__(self, items):
        self.items = items
        self.i = 0

    def next(self):
        b = self.items[self.i % len(self.items)]
        self.i += 1
        return b


class K:
    def __init__(self, nc, es):
        self.nc = nc
        self.es = es
        self.stack = [es]
        self.engs = {"pe": nc.tensor, "act": nc.scalar, "dve": nc.vector, "pool": nc.gpsimd, "sp": nc.sync}
        self.semobj = {}
        self.total = {}
        self.seen = {e: {} for e in self.engs}
        for e in self.engs:
            key = "E:" + e
            self.semobj[key] = es.enter_context(nc.semaphore("s_" + e))
            self.total[key] = 0
        self.nwaits = 0
        self.nins = 0
        self.uid = 0

    def push(self):
        s = ExitStack()
        s.__enter__()
        self.stack.append(s)

    def pop(self):
        self.barrier()
        s = self.stack.pop()
        s.__exit__(None, None, None)

    def sb(self, name, shape, dtype, dkey=None):
        self.uid += 1
        b = Buf(name, self.stack[-1].enter_context(self.nc.sbuf_tensor("%s_%d" % (name, self.uid), shape, dtype)))
        b.dkey = self.dsem(dkey if dkey else name)
        return b

    def ps(self, name, shape, dtype):
        self.uid += 1
        return Buf(name, self.stack[-1].enter_context(self.nc.psum_tensor("%s_%d" % (name, self.uid), shape, dtype)))

    def pool(self, kind, name, n, shape, dtype, dkey=None):
        f = self.sb if kind == "sb" else self.ps
        if kind == "sb":
            return RR([f("%s%d" % (name, i), shape, dtype, dkey=dkey) for i in range(n)])
        return RR([f("%s%d" % (name, i), shape, dtype) for i in range(n)])

    def dsem(self, name):
        key = "D:" + name
        if key not in self.semobj:
            self.semobj[key] = None
            self.total[key] = 0
        return key

    def _sem(self, key):
        if self.semobj[key] is None:
            self.semobj[key] = self.es.enter_context(self.nc.semaphore("d_" + key[2:]))
        return self.semobj[key]

    def _deps(self, reads, writes, join):
        deps = {}
        for b in reads:
            for k, v in b.w.items():
                if deps.get(k, 0) < v:
                    deps[k] = v
        for b in writes:
            if not join:
                for k, v in b.w.items():
                    if deps.get(k, 0) < v:
                        deps[k] = v
            for k, v in b.r.items():
                if deps.get(k, 0) < v:
                    deps[k] = v
        return deps

    def _wait(self, eng, deps):
        e = self.engs[eng]
        seen = self.seen[eng]
        for k, v in deps.items():
            if k[0] == "D":
                v = self.total[k]
            elif eng == "pe" and k == "E:pe":
                continue
            if v == 0 or seen.get(k, 0) >= v:
                continue
            e.wait_ge(self._sem(k), v)
            seen[k] = v
            self.nwaits += 1

    def _mark(self, key, val, reads, writes, join):
        for b in reads:
            if b.r.get(key, 0) < val:
                b.r[key] = val
        for b in writes:
            if join:
                if b.w.get(key, 0) < val:
                    b.w[key] = val
            else:
                b.w = {key: val}
                b.r = {}

    def op(self, eng, fn, reads=(), writes=(), inc=True, join=False):
        self._wait(eng, self._deps(reads, writes, join))
        ins = fn(self.engs[eng])
        key = "E:" + eng
        if inc:
            self.total[key] += 1
            ins.then_inc(self.semobj[key], 1)
            val = self.total[key]
        else:
            val = self.total[key] + 1
        self._mark(key, val, reads, writes, join)
        self.nins += 1
        return ins

    def dma(self, q, out, in_, reads=(), writes=(), join=False, key=None, **kw):
        self._wait(q, self._deps(reads, writes, join))
        if key is None:
            b = writes[0] if writes else reads[0]
            key = b.dkey
        else:
            key = self.dsem(key)
        if q == "pool":
            key = self.dsem(key[2:] + "_sw")
        self.engs[q].dma_start(out=out, in_=in_, **kw).then_inc(self._sem(key), 16)
        self.total[key] += 16
        self._mark(key, self.total[key], reads, writes, join)
        self.nins += 1

    def barrier(self):
        allk = {k: v for k, v in self.total.items() if v > 0}
        for e in self.engs:
            self._wait(e, allk)

    def finish(self):
        self.barrier()


def make_cfg(nch=8, nl=2, debug=False):
    L = N_META + 512 * nch
    chunks = [(0, N_META)] + [(N_META + 512 * i, 512) for i in range(nch)]
    return dict(nch=nch, nl=nl, L=L, chunks=chunks, nkt=1 + 4 * nch, debug=debug)


def chunk_tiles(n):
    return [(j * 128, min(128, n - j * 128)) for j in range((n + 127) // 128)]


def ktile(i):
    return (0, N_META) if i == 0 else (N_META + 128 * (i - 1), 128)


class Small:
    def __init__(self, k, njunk=2):
        self.junk = k.pool("sb", "junk", njunk, [128, 1024], BF16)
        self.ss = k.pool("sb", "ss", 4, [128, 1], F32)
        self.ms = k.pool("sb", "ms", 4, [128, 1], F32)
        self.rstd = k.pool("sb", "rstd", 6, [128, 1], F32)
        self.nh = k.sb("nh", [128, 1], F32)
        k.op("pool", lambda e: e.memset(self.nh.t[:], -0.5), writes=[self.nh])


def rms_rstd(k, S, src_ap, P, width, src_bufs):
    junk = S.junk.next()
    ss = S.ss.next()
    ms = S.ms.next()
    rstd = S.rstd.next()
    k.op("act", lambda e: e.activation(out=junk.t[0:P, 0:width], in_=src_ap, func=AF.Square, accum_out=ss.t[0:P, :]),
         reads=src_bufs, writes=[junk, ss])
    k.op("dve", lambda e: e.tensor_scalar(out=ms.t[0:P, :], in0=ss.t[0:P, :], scalar1=1.0 / width, scalar2=EPS,
                                          op0=ALU.mult, op1=ALU.add), reads=[ss], writes=[ms])
    k.op("pool", lambda e: e.tensor_tensor(out=rstd.t[0:P, :], in0=ms.t[0:P, :], in1=S.nh.t[0:P, :], op=ALU.pow),
         reads=[ms, S.nh], writes=[rstd])
    return rstd


def load_w(k, dst, src3, ncols):
    nsplit = (ncols + 2047) // 2048
    step = (ncols + nsplit - 1) // nsplit
    a = 0
    first = True
    while a < ncols:
        b = min(ncols, a + step)
        k.dma("pool", dst.t[:, :, a:b], src3[:, :, a:b], writes=[dst], join=not first)
        first = False
        a = b


def mm_group(k, out_ap, out_buf, pairs, rbufs, first_join=False):
    n = len(pairs)
    for i, (l, r) in enumerate(pairs):
        k.op("pe", lambda e: e.matmul(out_ap, lhsT=l, rhs=r, start=(i == 0), stop=(i == n - 1)),
             reads=rbufs, writes=[out_buf], inc=(i == n - 1), join=(first_join or i > 0))


def evac(k, eng, out_ap, in_ap, rbuf, wbuf, scale=None, join=False):
    if eng == "act":
        if scale is None:
            k.op("act", lambda e: e.activation(out=out_ap, in_=in_ap, func=AF.Copy), reads=[rbuf], writes=[wbuf], join=join)
        else:
            k.op("act", lambda e: e.activation(out=out_ap, in_=in_ap, func=AF.Copy, scale=scale), reads=[rbuf],
                 writes=[wbuf], join=join)
    else:
        if scale is None:
            k.op("dve", lambda e: e.tensor_copy(out=out_ap, in_=in_ap), reads=[rbuf], writes=[wbuf], join=join)
        else:
            k.op("dve", lambda e: e.tensor_scalar(out=out_ap, in0=in_ap, scalar1=scale, scalar2=None, op0=ALU.mult),
                 reads=[rbuf], writes=[wbuf], join=join)


def transposes(k, C, pT, src, P, col_offs, nslot0=0):
    n = len(col_offs)
    for i, off in enumerate(col_offs):
        k.op("pe", lambda e: e.transpose(out=pT.t[:, nslot0 + i, 0:P], in_=src.t[0:P, off:off + 128],
                                         identity=C["ident"].t[0:P, 0:P]),
             reads=[src, C["ident"]], writes=[pT], inc=(i == n - 1), join=(i > 0))


def phase_A(k, cfg, l, D, C, VS):
    k.push()
    S = Small(k, njunk=1)
    win = k.sb("win", [128, 8, 2240], BF16)
    wuq = k.sb("wuq", [128, 3, 1024], BF16)
    wukv = k.sb("wukv", [128, 2, 1024], BF16)
    wv_ = D["win"][l].rearrange("(c p) n -> p c n", p=128)
    winA = Buf("winA", win.t)
    winA.dkey = k.dsem("winA")
    k.dma("pool", win.t[:, :, 1024:2240], wv_[:, :, 1024:2240], writes=[winA])
    k.dma("pool", win.t[:, :, 0:1024], wv_[:, :, 0:1024], writes=[win])
    load_w(k, wuq, D["wuq"][l].rearrange("(c p) n -> p c n", p=128), 1024)
    load_w(k, wukv, D["wukv"][l].rearrange("(c p) n -> p c n", p=128), 1024)
    gpre = k.sb("gA", [128, 1024], F32)
    glat = k.sb("gB", [128, 640], F32)
    k.dma("sp", gpre.t[:], D["gpre"][l], writes=[gpre])
    k.dma("sp", glat.t[:], D["glat"][l], writes=[glat])
    hsrc = D["hin"] if l == 0 else D["H2"]

    hA = k.pool("sb", "h", 4, [128, 1024], F32)
    ub = k.pool("sb", "ub", 4, [128, 1024], BF16)
    uTp = k.pool("sb", "uT", 2, [128, 8, 512], BF16)
    latTp = k.pool("sb", "latT", 2, [128, 5, 512], BF16)
    latb = k.pool("sb", "latb", 2, [128, 640], BF16)
    rqp = k.pool("sb", "rq", 2, [128, 2, 512], F32)
    rkp = k.pool("sb", "rk", 2, [32, 2, 512], F32)
    ob = k.pool("sb", "ob", 8, [128, 512], BF16)
    t1p = k.pool("sb", "t1", 2, [128, 512], F32)
    t2p = k.pool("sb", "t2", 2, [128, 512], F32)
    pT = k.pool("ps", "pT", 2, [128, 8, 128], BF16)
    pm = k.pool("ps", "pm", 2, [128, 512], F32)
    pl = k.pool("ps", "pl", 2, [128, 1024], F32)
    ev = RR(["act", "dve"])

    CH = cfg["chunks"]
    cs = [dict() for _ in CH]

    def Nload(ci):
        t0, n = CH[ci]
        c = cs[ci]
        c["uT"] = uTp.next()
        c["rq"] = rqp.next()
        c["rk"] = rkp.next()
        c["us"] = {}
        k.dma("sp", c["rq"].t[:, :, 0:n], D["ropeq"][:, :, t0:t0 + n], writes=[c["rq"]])
        k.dma("sp", c["rk"].t[:, :, 0:n], D["ropek"][:, :, t0:t0 + n], writes=[c["rk"]])

    def N1load(ci, j):
        t0, n = CH[ci]
        o, P = chunk_tiles(n)[j]
        h = hA.next()
        cs[ci].setdefault("hs", {})[j] = h
        k.dma("sp", h.t[0:P, :], hsrc[t0 + o:t0 + o + P, :], writes=[h])

    def N1(ci, j):
        t0, n = CH[ci]
        o, P = chunk_tiles(n)[j]
        if j not in cs[ci].get("hs", {}):
            N1load(ci, j)
        h = cs[ci]["hs"][j]
        rstd = rms_rstd(k, S, h.t[0:P, :], P, 1024, [h])
        u = ub.next()
        cs[ci]["us"][j] = u
        k.op("dve", lambda e: e.scalar_tensor_tensor(out=u.t[0:P, :], in0=h.t[0:P, :], scalar=rstd.t[0:P, :],
                                                     in1=gpre.t[0:P, :], op0=ALU.mult, op1=ALU.mult),
             reads=[h, rstd, gpre], writes=[u])

    def N2(ci, j):
        t0, n = CH[ci]
        o, P = chunk_tiles(n)[j]
        uT_ = cs[ci]["uT"]
        p = pT.next()
        transposes(k, C, p, cs[ci]["us"][j], P, [c * 128 for c in range(8)])
        evac(k, "dve", uT_.t[:, :, o:o + P], p.t[:, :, 0:P], p, uT_, join=(j > 0))

    Nload(0)
    for j in range(len(chunk_tiles(CH[0][1]))):
        N1(0, j)
        N2(0, j)

    for ci, (t0, n) in enumerate(CH):
        tiles = chunk_tiles(n)
        kt0 = 0 if ci == 0 else 4 * (ci - 1) + 1
        nt = len(tiles)
        uT = cs[ci]["uT"]
        rq = cs[ci]["rq"]
        rk = cs[ci]["rk"]
        latT = latTp.next()
        nxt = ci + 1 if ci + 1 < len(CH) else None

        def sbqk(ct):
            p = pm.next()
            mm_group(k, p.t[:, 0:n], p,
                     [(win.t[:, c, ct * 128:(ct + 1) * 128], uT.t[:, c, 0:n]) for c in range(8)], [win, uT])
            o_ = ob.next()
            evac(k, "act", o_.t[:, 0:n], p.t[:, 0:n], p, o_, scale=(0.125 if ct < 4 else None))
            k.dma("act", D["QKsb"][ct * 128:(ct + 1) * 128, t0:t0 + n], o_.t[:, 0:n], reads=[o_])

        qs = {}
        lbs = {}

        def T1(j):
            o, P = tiles[j]
            p = pm.next()
            mm_group(k, p.t[0:P, :], p, [(uT.t[:, c, o:o + P], win.t[:, c, 1024:1536]) for c in range(8)], [winA, uT])
            kt = kt0 + j
            evac(k, "act", VS["vs"].t[0:P, kt, :], p.t[0:P, :], p, VS["vs"], join=True)
            q = pl.next()
            qs[j] = q
            mm_group(k, q.t[0:P, 0:384], q, [(uT.t[:, c, o:o + P], win.t[:, c, 1536:1920]) for c in range(8)], [winA, uT])
            mm_group(k, q.t[0:P, 512:768], q, [(uT.t[:, c, o:o + P], win.t[:, c, 1920:2176]) for c in range(8)],
                     [winA, uT], first_join=True)

        def T2(j):
            o, P = tiles[j]
            q = qs[j]
            r1 = rms_rstd(k, S, q.t[0:P, 0:384], P, 384, [q])
            r2 = rms_rstd(k, S, q.t[0:P, 512:768], P, 256, [q])
            lb = latb.next()
            lbs[j] = lb
            k.op("dve", lambda e: e.scalar_tensor_tensor(out=lb.t[0:P, 0:384], in0=q.t[0:P, 0:384], scalar=r1.t[0:P, :],
                                                         in1=glat.t[0:P, 0:384], op0=ALU.mult, op1=ALU.mult),
                 reads=[q, r1, glat], writes=[lb])
            k.op("dve", lambda e: e.scalar_tensor_tensor(out=lb.t[0:P, 384:640], in0=q.t[0:P, 512:768],
                                                         scalar=r2.t[0:P, :], in1=glat.t[0:P, 384:640], op0=ALU.mult,
                                                         op1=ALU.mult),
                 reads=[q, r2, glat], writes=[lb], join=True)

        def T3(j):
            o, P = tiles[j]
            p2 = pT.next()
            transposes(k, C, p2, lbs[j], P, [c * 128 for c in range(5)])
            evac(k, "dve", latT.t[:, :, o:o + P], p2.t[:, 0:5, 0:P], p2, latT, join=(j > 0))

        T1(0)
        if nxt is not None:
            Nload(nxt)
            for j in range(len(chunk_tiles(CH[nxt][1]))):
                N1load(nxt, j)
        for j in range(nt):
            if j + 1 < nt:
                T1(j + 1)
            T2(j)
            if j >= 1:
                T3(j - 1)
        ntn = len(chunk_tiles(CH[nxt][1])) if nxt is not None else 0
        for ct in range(4):
            sbqk(ct)
            if ct < ntn:
                N1(nxt, ct)
        T3(nt - 1)
        for ct in range(4, 8):
            sbqk(ct)
        pk = pm.next()
        mm_group(k, pk.t[0:32, 0:n], pk, [(win.t[:, c, 2176:2208], uT.t[:, c, 0:n]) for c in range(8)], [winA, uT])
        pks = pm.next()
        mm_group(k, pks.t[0:32, 0:n], pks, [(win.t[:, c, 2208:2240], uT.t[:, c, 0:n]) for c in range(8)], [winA, uT])
        t1 = t1p.next()
        t2 = t2p.next()
        k.op("dve", lambda e: e.tensor_tensor(out=t1.t[0:32, 0:n], in0=pk.t[0:32, 0:n], in1=rk.t[0:32, 0, 0:n], op=ALU.mult),
             reads=[pk, rk], writes=[t1])
        k.op("dve", lambda e: e.tensor_tensor(out=t2.t[0:32, 0:n], in0=pks.t[0:32, 0:n], in1=rk.t[0:32, 1, 0:n], op=ALU.mult),
             reads=[pks, rk], writes=[t2])
        o_ = ob.next()
        k.op("pool", lambda e: e.tensor_tensor(out=o_.t[0:32, 0:n], in0=t1.t[0:32, 0:n], in1=t2.t[0:32, 0:n], op=ALU.add),
             reads=[t1, t2], writes=[o_])
        k.dma("pool", D["KR"][:, t0:t0 + n], o_.t[0:32, 0:n], reads=[o_])
        if nxt is not None:
            for j in range(len(chunk_tiles(CH[nxt][1]))):
                N2(nxt, j)
        for ct in range(4):
            p = pm.next()
            mm_group(k, p.t[:, 0:n], p, [(wuq.t[:, c, ct * 128:(ct + 1) * 128], latT.t[:, c, 0:n]) for c in range(3)],
                     [wuq, latT])
            o_ = ob.next()
            evac(k, "act", o_.t[:, 0:n], p.t[:, 0:n], p, o_, scale=SCALE_MLA)
            k.dma("act", D["QMn"][ct * 128:(ct + 1) * 128, t0:t0 + n], o_.t[:, 0:n], reads=[o_])
        for rt in range(2):
            pr = pm.next()
            mm_group(k, pr.t[:, 0:n], pr,
                     [(wuq.t[:, c, 512 + rt * 128:512 + (rt + 1) * 128], latT.t[:, c, 0:n]) for c in range(3)], [wuq, latT])
            prs = pm.next()
            mm_group(k, prs.t[:, 0:n], prs,
                     [(wuq.t[:, c, 768 + rt * 128:768 + (rt + 1) * 128], latT.t[:, c, 0:n]) for c in range(3)], [wuq, latT])
            t1 = t1p.next()
            t2 = t2p.next()
            k.op("dve", lambda e: e.tensor_tensor(out=t1.t[:, 0:n], in0=pr.t[:, 0:n], in1=rq.t[:, 0, 0:n], op=ALU.mult),
                 reads=[pr, rq], writes=[t1])
            k.op("dve", lambda e: e.tensor_tensor(out=t2.t[:, 0:n], in0=prs.t[:, 0:n], in1=rq.t[:, 1, 0:n], op=ALU.mult),
                 reads=[prs, rq], writes=[t2])
            o_ = ob.next()
            k.op("pool", lambda e: e.tensor_tensor(out=o_.t[:, 0:n], in0=t1.t[:, 0:n], in1=t2.t[:, 0:n], op=ALU.add),
                 reads=[t1, t2], writes=[o_])
            k.dma("pool", D["QMr"][rt * 128:(rt + 1) * 128, t0:t0 + n], o_.t[:, 0:n], reads=[o_])
        for ct in range(4):
            p = pm.next()
            mm_group(k, p.t[:, 0:n], p, [(wukv.t[:, c, ct * 128:(ct + 1) * 128], latT.t[:, 3 + c, 0:n]) for c in range(2)],
                     [wukv, latT])
            o_ = ob.next()
            evac(k, "act", o_.t[:, 0:n], p.t[:, 0:n], p, o_)
            k.dma("act", D["KMn"][ct * 128:(ct + 1) * 128, t0:t0 + n], o_.t[:, 0:n], reads=[o_])
        for j, (o, P) in enumerate(tiles):
            p = pm.next()
            mm_group(k, p.t[0:P, :], p, [(latT.t[:, 3 + c, o:o + P], wukv.t[:, c, 512:1024]) for c in range(2)], [wukv, latT])
            kt = kt0 + j
            k.op("act", lambda e: e.activation(out=VS["vm65"].t[0:P, kt, :, 0:64],
                                               in_=p.t[0:P, :].rearrange("p (h d) -> p h d", h=8), func=AF.Copy),
                 reads=[p], writes=[VS["vm65"]], join=True)
    k.pop()


def chunk_geom(cfg, ci, i):
    if ci == 0:
        return 0, True
    m = i - (4 * (ci - 1) + 1)
    if m >= 0:
        return 128 * m, True
    return 0, False


def phase_B(k, cfg, l, D, C, last, VS):
    k.push()
    L = cfg["L"]
    chunks = cfg["chunks"]
    vs, vm65 = VS["vs"], VS["vm65"]
    qp = k.pool("sb", "qT", 2, [96, L], BF16)
    kp = k.pool("sb", "kT", 2, [96, L], BF16)
    ep = k.pool("sb", "e", 2, [128, 512], F32)
    spp = k.pool("sb", "sp", 3, [128, 512], BF16)
    ptp = k.pool("sb", "pt", 3, [128, 512], BF16)
    S32 = [k.sb("S32_%d" % i, [128, 512], F32) for i in range(2)]
    Sbf = [k.sb("Sbf%d" % i, [128, 512], BF16) for i in range(2)]
    osb = k.pool("sb", "osb", 2, [128, 4, 64], F32)
    lsb = k.pool("sb", "lsb", 2, [128, 4], F32)
    rlp = k.pool("sb", "rl", 2, [128, 4], F32)
    pa_p = k.pool("ps", "pa", 5, [128, 512], F32)
    po_p = k.pool("ps", "po", 2, [128, 512], F32)
    pdum = k.ps("pdum", [128, 512], F32)
    NDUM = cfg.get("ndum", 2)
    ident, ntri, nones, msb, mmla, zeros = (C[x] for x in ("ident", "ntri", "nones", "msb", "mmla", "zeros"))

    def dummies(nd):
        for _ in range(nd):
            k.op("pe", lambda e: e.matmul(pdum.t[:, :], lhsT=zeros.t[:, 0:128], rhs=zeros.t[:, 0:512], start=True, stop=True),
                 reads=[zeros], writes=[pdum], inc=False, join=True)

    def store_o(o_, ci, t0, n, col0):
        if ci == 0:
            k.dma("sp", D["Ocat"][0:N_META, col0:col0 + 64], o_.t[0:N_META, 0, :], reads=[o_])
        else:
            k.dma("sp", D["Ocat"][t0:t0 + n, col0:col0 + 64].rearrange("(j p) d -> p j d", p=128), o_.t[:, :, :], reads=[o_])

    def make_pairs(reverse):
        prs = []
        for h in range(8):
            for ci, (t0, n) in enumerate(chunks):
                if last and ci == 0:
                    continue
                kts = [0] if ci == 0 else (list(range(4 * ci, -1, -1)) if reverse else list(range(0, 4 * ci + 1)))
                for x, i in enumerate(kts):
                    ktok, ksz = ktile(i)
                    c0, diag = chunk_geom(cfg, ci, i)
                    prs.append(dict(h=h, ci=ci, t0=t0, n=n, x=x, first=(x == 0), lastk=(x == len(kts) - 1), i=i,
                                    ktok=ktok, ksz=ksz, c0=c0, diag=diag, dw=min(128, n - c0), qtiles=chunk_tiles(n)))
        return prs

    prs = make_pairs(True)
    N = len(prs)
    heads = {}
    cst = {}

    def head_tiles(h):
        if h not in heads:
            qT = qp.next()
            kT = kp.next()
            k.dma("sp", qT.t[0:64, :], D["QKsb"][h * 64:(h + 1) * 64, :], writes=[qT])
            k.dma("sp", kT.t[0:64, :], D["QKsb"][512 + h * 64:512 + (h + 1) * 64, :], writes=[kT])
            heads[h] = (qT, kT)
        return heads[h]

    def PE1(y):
        s = prs[y]
        qT, kT = head_tiles(s["h"])
        pa = pa_p.next()
        s["pa"] = pa
        ksz, c0, ktok, t0, n = s["ksz"], s["c0"], s["ktok"], s["t0"], s["n"]
        k.op("pe", lambda e: e.matmul(pa.t[0:ksz, c0:n], lhsT=kT.t[0:64, ktok:ktok + ksz], rhs=qT.t[0:64, t0 + c0:t0 + n],
                                      start=True, stop=True, skip_group_check=True),
             reads=[kT, qT], writes=[pa], inc=not s["diag"])
        if s["diag"]:
            dw = s["dw"]
            k.op("pe", lambda e: e.matmul(pa.t[0:ksz, c0:c0 + dw], lhsT=ident.t[0:ksz, 0:ksz], rhs=msb.t[0:ksz, 0:dw],
                                          start=False, stop=True, skip_group_check=True),
                 reads=[ident, msb], writes=[pa], join=True)

    def A1(y):
        s = prs[y]
        e_ = ep.next()
        s["e"] = e_
        ksz, c0, pa, n = s["ksz"], s["c0"], s["pa"], s["n"]
        k.op("act", lambda e: e.activation(out=e_.t[0:ksz, c0:n], in_=pa.t[0:ksz, c0:n], func=AF.Exp),
             reads=[pa], writes=[e_])

    def A2(y):
        s = prs[y]
        sp = spp.next()
        s["sp"] = sp
        ksz, c0, e_, n = s["ksz"], s["c0"], s["e"], s["n"]
        k.op("act", lambda e: e.activation(out=sp.t[0:ksz, c0:n], in_=e_.t[0:ksz, c0:n], func=AF.Ln, bias=1.0),
             reads=[e_], writes=[sp])

    def V(y):
        s = prs[y]
        if s["lastk"]:
            return
        c0, sp, n = s["c0"], s["sp"], s["n"]
        s32 = S32[s["ci"] % 2]
        if s["first"]:
            k.op("pool", lambda e: e.memset(s32.t[:], 0.0), writes=[s32])
        sb_ = Sbf[y % 2]
        k.op("dve", lambda e: e.tensor_tensor(out=s32.t[:, c0:n], in0=s32.t[:, c0:n], in1=sp.t[:, c0:n], op=ALU.add),
             reads=[sp, s32], writes=[s32])
        k.op("dve", lambda e: e.tensor_copy(out=sb_.t[:, c0:n], in_=s32.t[:, c0:n]), reads=[s32], writes=[sb_])

    def PE2(y):
        s = prs[y]
        ksz, c0, pa, sp, n = s["ksz"], s["c0"], s["pa"], s["sp"], s["n"]
        cS0 = (c0 + 128) if s["diag"] else 0
        useS = (s["x"] > 0) and (cS0 < n)
        k.op("pe", lambda e: e.matmul(pa.t[0:ksz, c0:n], lhsT=ntri.t[0:ksz, 0:ksz], rhs=sp.t[0:ksz, c0:n],
                                      start=False, stop=True, skip_group_check=True),
             reads=[ntri, sp], writes=[pa], inc=not useS, join=True)
        if useS:
            sb_ = Sbf[(y - 1) % 2]
            k.op("pe", lambda e: e.matmul(pa.t[0:ksz, cS0:n], lhsT=nones.t[:, 0:ksz], rhs=sb_.t[:, cS0:n],
                                          start=False, stop=True, skip_group_check=True),
                 reads=[nones, sb_], writes=[pa], join=True)

    def A3(y):
        s = prs[y]
        pt = ptp.next()
        s["pt"] = pt
        ksz, c0, pa, n = s["ksz"], s["c0"], s["pa"], s["n"]
        k.op("act", lambda e: e.activation(out=pt.t[0:ksz, c0:n], in_=pa.t[0:ksz, c0:n], func=AF.Exp),
             reads=[pa], writes=[pt])

    def PE3(y):
        s = prs[y]
        ksz, c0, pt, i, h, qtiles = s["ksz"], s["c0"], s["pt"], s["i"], s["h"], s["qtiles"]
        key = (h, s["ci"])
        if s["first"]:
            po = po_p.next()
            cst[key] = po
            k.op("pe", lambda e: e.matmul(po.t[:, 0:256], lhsT=zeros.t[:, 0:128], rhs=zeros.t[:, 0:256], start=True,
                                          stop=False, skip_group_check=True), reads=[zeros], writes=[po], inc=False)
        po = cst[key]
        jqs = [jq for jq, (qo, qw) in enumerate(qtiles) if qo >= c0]
        for jq in jqs:
            qo, qw = qtiles[jq]
            k.op("pe", lambda e: e.matmul(po.t[0:qw, jq * 64:(jq + 1) * 64], lhsT=pt.t[0:ksz, qo:qo + qw],
                                          rhs=vs.t[0:ksz, i, h * 64:(h + 1) * 64], start=False, stop=s["lastk"],
                                          skip_group_check=True),
                 reads=[pt, vs], writes=[po], inc=(jq == jqs[-1]), join=True)
        if s["lastk"]:
            Pq = qtiles[0][1]
            nt = len(qtiles)
            o_ = osb.next()
            k.op("dve", lambda e: e.tensor_copy(out=o_.t[0:Pq, 0:nt, :],
                                                in_=po.t[0:Pq, 0:nt * 64].rearrange("p (j d) -> p j d", d=64)),
                 reads=[po], writes=[o_])
            store_o(o_, s["ci"], s["t0"], s["n"], h * 64)

    PE1(0)
    A1(0)
    if N > 1:
        PE1(1)
    for y in range(N):
        A2(y)
        V(y)
        if y + 1 < N:
            A1(y + 1)
        PE2(y)
        if y + 2 < N:
            PE1(y + 2)
        if y >= 1:
            A3(y - 1)
            dummies(NDUM)
            PE3(y - 1)
    A3(N - 1)
    PE3(N - 1)

    prs = make_pairs(False)
    N = len(prs)
    heads = {}
    cst = {}

    def mla_tiles(h):
        if h not in heads:
            qT = qp.next()
            kT = kp.next()
            k.dma("sp", qT.t[0:64, :], D["QMn"][h * 64:(h + 1) * 64, :], writes=[qT])
            k.dma("sp", qT.t[64:96, :], D["QMr"][h * 32:(h + 1) * 32, :], writes=[qT], join=True)
            k.dma("sp", kT.t[0:64, :], D["KMn"][h * 64:(h + 1) * 64, :], writes=[kT])
            k.dma("sp", kT.t[64:96, :], D["KR"][:, :], writes=[kT], join=True)
            heads[h] = (qT, kT)
        return heads[h]

    def M1(y):
        s = prs[y]
        qT, kT = mla_tiles(s["h"])
        pa = pa_p.next()
        s["pa"] = pa
        ksz, c0, ktok, t0, n = s["ksz"], s["c0"], s["ktok"], s["t0"], s["n"]
        k.op("pe", lambda e: e.matmul(pa.t[0:ksz, c0:n], lhsT=kT.t[:, ktok:ktok + ksz], rhs=qT.t[:, t0 + c0:t0 + n],
                                      start=True, stop=not s["diag"]), reads=[kT, qT], writes=[pa], inc=not s["diag"])
        if s["diag"]:
            dw = s["dw"]
            k.op("pe", lambda e: e.matmul(pa.t[0:ksz, c0:c0 + dw], lhsT=ident.t[0:ksz, 0:ksz], rhs=mmla.t[0:ksz, 0:dw],
                                          start=False, stop=True), reads=[ident, mmla], writes=[pa], join=True)

    def MA(y):
        s = prs[y]
        pt = ptp.next()
        s["pt"] = pt
        ksz, c0, pa, n = s["ksz"], s["c0"], s["pa"], s["n"]
        k.op("act", lambda e: e.activation(out=pt.t[0:ksz, c0:n], in_=pa.t[0:ksz, c0:n], func=AF.Exp),
             reads=[pa], writes=[pt])

    def M3(y):
        s = prs[y]
        ksz, c0, pt, i, h, qtiles = s["ksz"], s["c0"], s["pt"], s["i"], s["h"], s["qtiles"]
        key = (h, s["ci"])
        if s["first"]:
            po = po_p.next()
            cst[key] = po
            k.op("pe", lambda e: e.matmul(po.t[:, 0:260], lhsT=zeros.t[:, 0:128], rhs=zeros.t[:, 0:260], start=True,
                                          stop=False, skip_group_check=True), reads=[zeros], writes=[po], inc=False)
        po = cst[key]
        jqs = [jq for jq, (qo, qw) in enumerate(qtiles) if qo >= c0]
        for jq in jqs:
            qo, qw = qtiles[jq]
            k.op("pe", lambda e: e.matmul(po.t[0:qw, jq * 65:jq * 65 + 65], lhsT=pt.t[0:ksz, qo:qo + qw],
                                          rhs=vm65.t[0:ksz, i, h, :], start=False, stop=s["lastk"],
                                          skip_group_check=True),
                 reads=[pt, vm65], writes=[po], inc=(jq == jqs[-1]), join=True)
        if s["lastk"]:
            Pq = qtiles[0][1]
            nt = len(qtiles)
            ls = lsb.next()
            rl = rlp.next()
            o_ = osb.next()
            pov = po.t[0:Pq, 0:nt * 65].rearrange("p (j c) -> p j c", c=65)
            k.op("dve", lambda e: e.tensor_copy(out=ls.t[0:Pq, 0:nt], in_=pov[:, :, 64]), reads=[po], writes=[ls])
            k.op("dve", lambda e: e.reciprocal(out=rl.t[0:Pq, 0:nt], in_=ls.t[0:Pq, 0:nt]), reads=[ls], writes=[rl])
            for jq in range(nt):
                k.op("dve", lambda e: e.tensor_scalar(out=o_.t[0:Pq, jq, :], in0=po.t[0:Pq, jq * 65:jq * 65 + 64],
                                                      scalar1=rl.t[0:Pq, jq:jq + 1], scalar2=None, op0=ALU.mult),
                     reads=[po, rl], writes=[o_], join=(jq > 0))
            store_o(o_, s["ci"], s["t0"], s["n"], 512 + h * 64)

    M1(0)
    if N > 1:
        M1(1)
    for y in range(N):
        MA(y)
        if y + 2 < N:
            M1(y + 2)
        M3(y)
    k.pop()


def phase_C(k, cfg, l, D, C, last, W=None):
    k.push()
    S = Small(k)
    wo = k.sb("wo", [128, 8, 1024], BF16)
    load_w(k, wo, D["wo"][l].rearrange("(c p) n -> p c n", p=128), 1024)
    gout = k.sb("gA", [128, 1024], F32)
    gpost = k.sb("gB", [128, 1024], F32)
    k.dma("sp", gout.t[:], D["gout"][l], writes=[gout])
    k.dma("sp", gpost.t[:], D["gpost"][l], writes=[gpost])
    hsrc = D["hin"] if l == 0 else D["H2"]
    ocp = k.pool("sb", "oc", 2, [128, 1024], F32)
    hp = k.pool("sb", "h", 3, [128, 1024], F32)
    mbp = k.pool("sb", "mb", 2, [128, 1024], BF16)
    mTp = k.pool("sb", "mT", 2, [128, 8, 128], BF16)
    ttp = k.pool("sb", "tt", 2, [128, 1024], F32)
    pT = k.pool("ps", "pT", 2, [128, 8, 128], BF16)
    pmix = k.pool("ps", "pmix", 2, [128, 1024], F32)
    ev = RR(["act", "dve"])
    tl = []
    for ci, (t0c, n) in enumerate(cfg["chunks"]):
        if last and ci == 0:
            continue
        for (o, P) in chunk_tiles(n):
            tl.append((t0c + o, P))
    st = [dict() for _ in tl]

    def S1(j):
        t0, P = tl[j]
        oc = ocp.next()
        h = hp.next()
        st[j]["h"] = h
        k.dma("sp", oc.t[0:P, :], D["Ocat"][t0:t0 + P, :], writes=[oc])
        k.dma("sp", h.t[0:P, :], hsrc[t0:t0 + P, :], writes=[h])
        r1 = rms_rstd(k, S, oc.t[0:P, 0:512], P, 512, [oc])
        r2 = rms_rstd(k, S, oc.t[0:P, 512:1024], P, 512, [oc])
        mb = mbp.next()
        st[j]["mb"] = mb
        k.op("dve", lambda e: e.scalar_tensor_tensor(out=mb.t[0:P, 0:512], in0=oc.t[0:P, 0:512], scalar=r1.t[0:P, :],
                                                     in1=gout.t[0:P, 0:512], op0=ALU.mult, op1=ALU.mult),
             reads=[oc, r1, gout], writes=[mb])
        k.op("dve", lambda e: e.scalar_tensor_tensor(out=mb.t[0:P, 512:1024], in0=oc.t[0:P, 512:1024], scalar=r2.t[0:P, :],
                                                     in1=gout.t[0:P, 512:1024], op0=ALU.mult, op1=ALU.mult),
             reads=[oc, r2, gout], writes=[mb], join=True)

    def S2(j):
        t0, P = tl[j]
        mb = st[j]["mb"]
        p = pT.next()
        transposes(k, C, p, mb, P, [c * 128 for c in range(8)])
        mT = mTp.next()
        evac(k, "act", mT.t[:, :, 0:P], p.t[:, :, 0:P], p, mT)
        pmx = pmix.next()
        st[j]["pmx"] = pmx
        mm_group(k, pmx.t[0:P, 0:512], pmx, [(mT.t[:, c, 0:P], wo.t[:, c, 0:512]) for c in range(8)], [mT, wo])
        mm_group(k, pmx.t[0:P, 512:1024], pmx, [(mT.t[:, c, 0:P], wo.t[:, c, 512:1024]) for c in range(8)], [mT, wo],
                 first_join=True)

    def S3(j):
        t0, P = tl[j]
        pmx, h = st[j]["pmx"], st[j]["h"]
        r3 = rms_rstd(k, S, pmx.t[0:P, :], P, 1024, [pmx])
        tt = ttp.next()
        k.op("dve", lambda e: e.scalar_tensor_tensor(out=tt.t[0:P, :], in0=pmx.t[0:P, :], scalar=r3.t[0:P, :],
                                                     in1=gpost.t[0:P, :], op0=ALU.mult, op1=ALU.mult),
             reads=[pmx, r3, gpost], writes=[tt])
        k.op("dve", lambda e: e.tensor_tensor(out=tt.t[0:P, :], in0=tt.t[0:P, :], in1=h.t[0:P, :], op=ALU.add),
             reads=[tt, h], writes=[tt])
        k.dma("pool", D["H1"][t0:t0 + P, :], tt.t[0:P, :], reads=[tt])

    nT = len(tl)
    for j in range(nT + 2):
        if j == nT // 4:
            ffn_load(k, W, l, D)
        if j < nT:
            S1(j)
        if 0 <= j - 1 < nT:
            S2(j - 1)
        if 0 <= j - 2 < nT:
            S3(j - 2)
    k.pop()


def ffn_weights(k, cfg, l, D):
    NF = D_FF // 128
    W = {}
    W["wg"] = k.sb("wg", [128, 8, D_FF], BF16)
    W["wu"] = k.sb("wu", [128, 8, D_FF], BF16)
    W["wd"] = k.sb("wd", [128, NF, 1024], BF16)
    W["gpf"] = k.sb("gC", [128, 1024], F32)
    W["gpo"] = k.sb("gD", [128, 1024], F32)
    W["loaded"] = False
    return W


def ffn_load(k, W, l, D):
    if W is None or W["loaded"]:
        return
    W["loaded"] = True
    load_w(k, W["wg"], D["wg"][l].rearrange("(c p) n -> p c n", p=128), D_FF)
    load_w(k, W["wu"], D["wu"][l].rearrange("(c p) n -> p c n", p=128), D_FF)
    load_w(k, W["wd"], D["wd"][l].rearrange("(c p) n -> p c n", p=128), 1024)
    k.dma("sp", W["gpf"].t[:], D["gpf"][l], writes=[W["gpf"]])
    k.dma("sp", W["gpo"].t[:], D["gpo"][l], writes=[W["gpo"]])


def phase_D(k, cfg, l, D, C, last, W):
    k.push()
    S = Small(k, njunk=1)
    NF = D_FF // 128
    wg, wu, wd, gpf, gpo = W["wg"], W["wu"], W["wd"], W["gpf"], W["gpo"]
    xp = k.pool("sb", "x", 4, [128, 1024], F32)
    fbp = k.pool("sb", "ub", 2, [128, 1024], BF16)
    fTp = k.pool("sb", "fT", 2, [128, 8, 512], BF16)
    hT = k.sb("hT", [128, NF, 512], BF16)
    sgp = k.pool("sb", "sg", 1, [128, 512], F32)
    pT = k.pool("ps", "pT", 2, [128, 8, 128], BF16)
    pp = k.pool("ps", "pp", 6, [128, 512], F32)
    ev = RR(["act", "dve"])
    chs = [(ci, t0, n) for ci, (t0, n) in enumerate(cfg["chunks"]) if not (last and ci == 0)]
    fTs = {}

    def D1(x):
        ci, t0, n = chs[x]
        tiles = chunk_tiles(n)
        fT = fTp.next()
        fTs[x] = fT
        fbs = {}

        def N1(j):
            o, P = tiles[j]
            h = xp.next()
            k.dma("sp", h.t[0:P, :], D["H1"][t0 + o:t0 + o + P, :], writes=[h])
            rstd = rms_rstd(k, S, h.t[0:P, :], P, 1024, [h])
            fb = fbp.next()
            fbs[j] = fb
            k.op("dve", lambda e: e.scalar_tensor_tensor(out=fb.t[0:P, :], in0=h.t[0:P, :], scalar=rstd.t[0:P, :],
                                                         in1=gpf.t[0:P, :], op0=ALU.mult, op1=ALU.mult),
                 reads=[h, rstd, gpf], writes=[fb])

        def N2(j):
            o, P = tiles[j]
            p = pT.next()
            transposes(k, C, p, fbs[j], P, [c * 128 for c in range(8)])
            evac(k, ev.next(), fT.t[:, :, o:o + P], p.t[:, :, 0:P], p, fT, join=(j > 0))

        steps = [lambda: N1(0)]
        for j in range(len(tiles)):
            def step(j=j):
                if j + 1 < len(tiles):
                    N1(j + 1)
                N2(j)
            steps.append(step)
        return steps

    def D2(x, extra):
        ci, t0, n = chs[x]
        fT = fTs[x]
        for f in range(NF):
            if f >= 6 and f % 3 == 0 and extra:
                extra.pop(0)()
            g_ = pp.next()
            u_ = pp.next()
            mm_group(k, g_.t[:, 0:n], g_, [(wg.t[:, c, f * 128:(f + 1) * 128], fT.t[:, c, 0:n]) for c in range(8)], [wg, fT])
            mm_group(k, u_.t[:, 0:n], u_, [(wu.t[:, c, f * 128:(f + 1) * 128], fT.t[:, c, 0:n]) for c in range(8)], [wu, fT])
            sg = sgp.next()
            k.op("act", lambda e: e.activation(out=sg.t[:, 0:n], in_=g_.t[:, 0:n], func=AF.Silu), reads=[g_], writes=[sg])
            k.op("dve", lambda e: e.tensor_tensor(out=hT.t[:, f, 0:n], in0=u_.t[:, 0:n], in1=sg.t[:, 0:n], op=ALU.mult),
                 reads=[u_, sg], writes=[hT], join=(f > 0))

    def D3(x):
        ci, t0, n = chs[x]
        tiles = chunk_tiles(n)
        for j, (o, P) in enumerate(tiles):
            dA = pp.next()
            dB = pp.next()
            mm_group(k, dA.t[0:P, :], dA, [(hT.t[:, f, o:o + P], wd.t[:, f, 0:512]) for f in range(NF)], [hT, wd])
            mm_group(k, dB.t[0:P, :], dB, [(hT.t[:, f, o:o + P], wd.t[:, f, 512:1024]) for f in range(NF)], [hT, wd])
            hr = xp.next()
            k.dma("sp", hr.t[0:P, :], D["H1"][t0 + o:t0 + o + P, :], writes=[hr])
            tt = xp.next()
            k.op("act", lambda e: e.activation(out=tt.t[0:P, 0:512], in_=dA.t[0:P, :], func=AF.Copy), reads=[dA], writes=[tt])
            k.op("act", lambda e: e.activation(out=tt.t[0:P, 512:1024], in_=dB.t[0:P, :], func=AF.Copy), reads=[dB],
                 writes=[tt], join=True)
            r3 = rms_rstd(k, S, tt.t[0:P, :], P, 1024, [tt])
            k.op("dve", lambda e: e.scalar_tensor_tensor(out=tt.t[0:P, :], in0=tt.t[0:P, :], scalar=r3.t[0:P, :],
                                                         in1=gpo.t[0:P, :], op0=ALU.mult, op1=ALU.mult),
                 reads=[tt, r3, gpo], writes=[tt])
            k.op("pool", lambda e: e.tensor_tensor(out=tt.t[0:P, :], in0=tt.t[0:P, :], in1=hr.t[0:P, :], op=ALU.add),
                 reads=[tt, hr], writes=[tt])
            if last:
                k.dma("pool", D["out"][t0 + o - N_META:t0 + o - N_META + P, :], tt.t[0:P, :], reads=[tt])
            else:
                k.dma("pool", D["H2"][t0 + o:t0 + o + P, :], tt.t[0:P, :], reads=[tt])

    for st_ in D1(0):
        st_()
    for x in range(len(chs)):
        extra = D1(x + 1) if x + 1 < len(chs) else []
        D2(x, extra)
        for st_ in extra:
            st_()
        D3(x)
    k.pop()


IN_SPECS = None


def in_specs(cfg):
    NL, L = cfg["nl"], cfg["L"]
    return [
        ("hin", [L, 1024], F32), ("win", [NL, 1024, 2240], F32), ("wuq", [NL, Q_LORA, 1024], F32),
        ("wukv", [NL, KV_LORA, 1024], F32), ("wo", [NL, 1024, 1024], F32), ("wg", [NL, 1024, D_FF], F32),
        ("wu", [NL, 1024, D_FF], F32), ("wd", [NL, D_FF, 1024], F32),
        ("gpre", [NL, 128, 1024], F32), ("glat", [NL, 128, 640], F32), ("gout", [NL, 128, 1024], F32),
        ("gpost", [NL, 128, 1024], F32), ("gpf", [NL, 128, 1024], F32), ("gpo", [NL, 128, 1024], F32),
        ("ropeq", [128, 2, L], F32), ("ropek", [32, 2, L], F32),
        ("ident", [128, 128], BF16), ("ntri", [128, 128], BF16), ("nones", [128, 128], BF16),
        ("msb", [128, 128], BF16), ("mmla", [128, 128], BF16), ("zeros", [128, 512], BF16), ("ones", [128, 8], BF16),
    ]


def scratch_specs(cfg):
    L = cfg["L"]
    return [
        ("QKsb", [1024, L], BF16), ("Vsb", [L, 512], BF16), ("QMn", [512, L], BF16), ("QMr", [256, L], BF16),
        ("KMn", [512, L], BF16), ("KR", [32, L], BF16), ("Vm", [L, 512], BF16), ("Ocat", [L, 1024], F32),
        ("H1", [L, 1024], F32), ("H2", [L, 1024], F32),
    ]


def build_program(cfg, phases=None):
    nc = bass.Bass("TRN2", target_bir_lowering=False)
    D = {}
    for name, shape, dt in in_specs(cfg):
        D[name] = nc.dram_tensor(name, shape, dt, kind="ExternalInput").ap()
    for name, shape, dt in scratch_specs(cfg):
        D[name] = nc.dram_tensor(name, shape, dt, kind=("ExternalOutput" if cfg["debug"] else "Internal")).ap()
    D["out"] = nc.dram_tensor("out", [cfg["L"] - N_META, 1024], F32, kind="ExternalOutput").ap()
    with ExitStack() as es:
        k = K(nc, es)
        C = {}
        for name, shape, dt in in_specs(cfg):
            if name in ("ident", "ntri", "nones", "msb", "mmla", "zeros", "ones"):
                C[name] = k.sb("c_" + name, shape, dt, dkey="consts")
                k.dma("sp", C[name].t[:], D[name], writes=[C[name]])
        for l in range(cfg["nl"]):
            last = (l == cfg["nl"] - 1) and not cfg["debug"]
            k.push()
            VS = dict(vs=k.sb("vs", [128, cfg["nkt"], 512], BF16), vm65=k.sb("vm65", [128, cfg["nkt"], 8, 65], BF16))
            k.op("pool", lambda e: e.memset(VS["vm65"].t[:, :, :, 64:65], 1.0), writes=[VS["vm65"]])
            k.op("pool", lambda e: e.memset(VS["vs"].t[:, 0, :], 0.0), writes=[VS["vs"]])
            if phases is None or "A" in phases:
                phase_A(k, cfg, l, D, C, VS)
            if phases is None or "B" in phases:
                phase_B(k, cfg, l, D, C, last, VS)
            k.pop()
            k.push()
            W = ffn_weights(k, cfg, l, D) if (phases is None or "D" in phases) else None
            if phases is None or "C" in phases:
                phase_C(k, cfg, l, D, C, last, W)
            if phases is None or "D" in phases:
                ffn_load(k, W, l, D)
                phase_D(k, cfg, l, D, C, (l == cfg["nl"] - 1), W)
            k.pop()
        k.finish()
        build_program.stats = (k.nins, k.nwaits)
    return nc


def host_shared(cfg, inp):
    NL, L = cfg["nl"], cfg["L"]
    f32 = np.float32
    bf = ml_dtypes.bfloat16
    o1, o2, o3 = 3 * SBW, 3 * SBW + Q_LORA, 3 * SBW + Q_LORA + KV_LORA
    w_in = np.asarray(inp["w_in"], f32)[:NL]
    win = np.concatenate([w_in, w_in[:, :, o3 + 16:o3 + 32], w_in[:, :, o3:o3 + 16]], axis=2)
    w_uq = np.asarray(inp["w_uq"], f32)[:NL].reshape(NL, Q_LORA, 8, 96)
    nope = w_uq[..., :64].reshape(NL, Q_LORA, 512)
    rope = w_uq[..., 64:].reshape(NL, Q_LORA, 256)
    ropes = np.concatenate([w_uq[..., 80:96], w_uq[..., 64:80]], axis=-1).reshape(NL, Q_LORA, 256)
    wuq = np.concatenate([nope, rope, ropes], axis=2)
    w_ukv = np.asarray(inp["w_ukv"], f32)[:NL].reshape(NL, KV_LORA, 8, 128)
    wukv = np.concatenate([w_ukv[..., :64].reshape(NL, KV_LORA, 512), w_ukv[..., 64:].reshape(NL, KV_LORA, 512)], axis=2)

    def rep(g):
        g = np.asarray(g, f32)[:NL]
        return np.ascontiguousarray(np.broadcast_to(g[:, None, :], (NL, 128, g.shape[-1])))

    inv_freq = (1.0 / (np.float32(10000.0) ** (np.arange(0, ROPE, 2, dtype=f32) / np.float32(ROPE)))).astype(f32)
    ang = (np.arange(L, dtype=f32)[:, None] * inv_freq[None, :]).astype(f32)
    cos = np.cos(ang).astype(f32).T
    sin = np.sin(ang).astype(f32).T
    c2 = np.concatenate([cos, cos], axis=0)
    s2 = np.concatenate([-sin, sin], axis=0)
    ropek = np.ascontiguousarray(np.stack([c2, s2], axis=1)).astype(f32)
    ropeq = np.ascontiguousarray(np.tile(ropek, (4, 1, 1)) * np.float32(SCALE_MLA)).astype(f32)
    j = np.arange(128)[:, None]
    s = np.arange(128)[None, :]
    sh = dict(
        win=np.ascontiguousarray(win), wuq=np.ascontiguousarray(wuq), wukv=np.ascontiguousarray(wukv),
        wo=np.ascontiguousarray(np.asarray(inp["w_o"], f32)[:NL]), wg=np.ascontiguousarray(np.asarray(inp["w_gate"], f32)[:NL]),
        wu=np.ascontiguousarray(np.asarray(inp["w_up"], f32)[:NL]), wd=np.ascontiguousarray(np.asarray(inp["w_down"], f32)[:NL]),
        gpre=rep(inp["pre_mix_norm"]),
        glat=rep(np.concatenate([np.asarray(inp["q_lat_norm"], f32), np.asarray(inp["kv_lat_norm"], f32)], axis=1)),
        gout=rep(np.concatenate([np.asarray(inp["sb_out_norm"], f32), np.asarray(inp["mla_out_norm"], f32)], axis=1)),
        gpost=rep(inp["post_mix_norm"]), gpf=rep(inp["pre_ffn_norm"]), gpo=rep(inp["post_ffn_norm"]),
        ropeq=ropeq, ropek=ropek,
        ident=np.eye(128, dtype=f32).astype(bf),
        ntri=np.where(j >= s, -1.0, 0.0).astype(f32).astype(bf),
        nones=np.full((128, 128), -1.0, f32).astype(bf),
        msb=np.where(j < s, 0.0, NEG).astype(f32).astype(bf),
        mmla=np.where(j <= s, 0.0, NEG).astype(f32).astype(bf),
        zeros=np.zeros((128, 512), f32).astype(bf),
        ones=np.ones((128, 8), f32).astype(bf),
    )
    return sh


_CACHE = {}


def kernel(**inputs):
    cfg = make_cfg()
    x = np.asarray(inputs["x"], np.float32)
    B = x.shape[0]
    meta = np.asarray(inputs["meta_tokens"], np.float32)
    sh = host_shared(cfg, inputs)
    in_maps = []
    for b in range(B):
        m = dict(sh)
        m["hin"] = np.ascontiguousarray(np.concatenate([meta, x[b]], axis=0))
        in_maps.append(m)
    if "nc" not in _CACHE:
        _CACHE["nc"] = build_program(cfg)
    res = run_bass_kernel_spmd(_CACHE["nc"], in_maps, core_ids=list(range(B)))
    return np.stack([np.asarray(r["out"], np.float32) for r in res.results], axis=0)
```

```python
from contextlib import ExitStack
import math

import numpy as np
import ml_dtypes
import concourse.bass as bass
import concourse.mybir as mybir
from concourse.bass_utils import run_bass_kernel_spmd

F32 = mybir.dt.float32
BF16 = mybir.dt.bfloat16
AF = mybir.ActivationFunctionType
ALU = mybir.AluOpType

D_MODEL = 1024
N_META = 16
SBW = 512
Q_LORA = 384
KV_LORA = 256
ROPE = 32
NOPE = 64
D_FF = 2816
EPS = 1e-6
NEG = -30000.0
SCALE_MLA = 1.0 / math.sqrt(96.0)


class Buf:
    __slots__ = ("name", "t", "w", "r", "dkey")

    def __init__(self, name, t):
        self.name = name
        self.t = t
        self.w = {}
        self.r = {}
        self.dkey = None


class RR:
    def __init__(self, items):
        self.items = items
        self.i = 0

    def next(self):
        b = self.items[self.i % len(self.items)]
        self.i += 1
        return b


class K:
    def __init__(self, nc, es):
        self.nc = nc
        self.es = es
        self.stack = [es]
        self.engs = {"pe": nc.tensor, "act": nc.scalar, "dve": nc.vector, "pool": nc.gpsimd, "sp": nc.sync}
        self.semobj = {}
        self.total = {}
        self.seen = {e: {} for e in self.engs}
        for e in self.engs:
            key = "E:" + e
            self.semobj[key] = es.enter_context(nc.semaphore("s_" + e))
            self.total[key] = 0
        self.nwaits = 0
        self.nins = 0
        self.uid = 0

    def push(self):
        s = ExitStack()
        s.__enter__()
        self.stack.append(s)

    def pop(self):
        self.barrier()
        s = self.stack.pop()
        s.__exit__(None, None, None)

    def sb(self, name, shape, dtype, dkey=None):
        self.uid += 1
        b = Buf(name, self.stack[-1].enter_context(self.nc.sbuf_tensor("%s_%d" % (name, self.uid), shape, dtype)))
        b.dkey = self.dsem(dkey if dkey else name)
        return b

    def ps(self, name, shape, dtype):
        self.uid += 1
        return Buf(name, self.stack[-1].enter_context(self.nc.psum_tensor("%s_%d" % (name, self.uid), shape, dtype)))

    def pool(self, kind, name, n, shape, dtype, dkey=None):
        f = self.sb if kind == "sb" else self.ps
        if kind == "sb":
            return RR([f("%s%d" % (name, i), shape, dtype, dkey=dkey) for i in range(n)])
        return RR([f("%s%d" % (name, i), shape, dtype) for i in range(n)])

    def dsem(self, name):
        key = "D:" + name
        if key not in self.semobj:
            self.semobj[key] = None
            self.total[key] = 0
        return key

    def _sem(self, key):
        if self.semobj[key] is None:
            self.semobj[key] = self.es.enter_context(self.nc.semaphore("d_" + key[2:]))
        return self.semobj[key]

    def _deps(self, reads, writes, join):
        deps = {}
        for b in reads:
            for k, v in b.w.items():
                if deps.get(k, 0) < v:
                    deps[k] = v
        for b in writes:
            if not join:
                for k, v in b.w.items():
                    if deps.get(k, 0) < v:
                        deps[k] = v
            for k, v in b.r.items():
                if deps.get(k, 0) < v:
                    deps[k] = v
        return deps

    def _wait(self, eng, deps):
        e = self.engs[eng]
        seen = self.seen[eng]
        for k, v in deps.items():
            if k[0] == "D":
                v = self.total[k]
            elif eng == "pe" and k == "E:pe":
                continue
            if v == 0 or seen.get(k, 0) >= v:
                continue
            e.wait_ge(self._sem(k), v)
            seen[k] = v
            self.nwaits += 1

    def _mark(self, key, val, reads, writes, join):
        for b in reads:
            if b.r.get(key, 0) < val:
                b.r[key] = val
        for b in writes:
            if join:
                if b.w.get(key, 0) < val:
                    b.w[key] = val
            else:
                b.w = {key: val}
                b.r = {}

    def op(self, eng, fn, reads=(), writes=(), inc=True, join=False):
        self._wait(eng, self._deps(reads, writes, join))
        ins = fn(self.engs[eng])
        key = "E:" + eng
        if inc:
            self.total[key] += 1
            ins.then_inc(self.semobj[key], 1)
            val = self.total[key]
        else:
            val = self.total[key] + 1
        self._mark(key, val, reads, writes, join)
        self.nins += 1
        return ins

    def dma(self, q, out, in_, reads=(), writes=(), join=False, key=None, **kw):
        self._wait(q, self._deps(reads, writes, join))
        if key is None:
            b = writes[0] if writes else reads[0]
            key = b.dkey
        else:
            key = self.dsem(key)
        if q == "pool":
            key = self.dsem(key[2:] + "_sw")
        self.engs[q].dma_start(out=out, in_=in_, **kw).then_inc(self._sem(key), 16)
        self.total[key] += 16
        self._mark(key, self.total[key], reads, writes, join)
        self.nins += 1

    def barrier(self):
        allk = {k: v for k, v in self.total.items() if v > 0}
        for e in self.engs:
            self._wait(e, allk)

    def finish(self):
        self.barrier()


def make_cfg(nch=8, nl=2, debug=False):
    L = N_META + 512 * nch
    chunks = [(0, N_META)] + [(N_META + 512 * i, 512) for i in range(nch)]
    return dict(nch=nch, nl=nl, L=L, chunks=chunks, nkt=1 + 4 * nch, debug=debug)


def chunk_tiles(n):
    return [(j * 128, min(128, n - j * 128)) for j in range((n + 127) // 128)]


def ktile(i):
    return (0, N_META) if i == 0 else (N_META + 128 * (i - 1), 128)


class Small:
    def __init__(self, k, njunk=2):
        self.junk = k.pool("sb", "junk", njunk, [128, 1024], BF16)
        self.ss = k.pool("sb", "ss", 4, [128, 1], F32)
        self.ms = k.pool("sb", "ms", 4, [128, 1], F32)
        self.rstd = k.pool("sb", "rstd", 6, [128, 1], F32)
        self.nh = k.sb("nh", [128, 1], F32)
        k.op("pool", lambda e: e.memset(self.nh.t[:], -0.5), writes=[self.nh])


def rms_rstd(k, S, src_ap, P, width, src_bufs):
    junk = S.junk.next()
    ss = S.ss.next()
    ms = S.ms.next()
    rstd = S.rstd.next()
    k.op("act", lambda e: e.activation(out=junk.t[0:P, 0:width], in_=src_ap, func=AF.Square, accum_out=ss.t[0:P, :]),
         reads=src_bufs, writes=[junk, ss])
    k.op("dve", lambda e: e.tensor_scalar(out=ms.t[0:P, :], in0=ss.t[0:P, :], scalar1=1.0 / width, scalar2=EPS,
                                          op0=ALU.mult, op1=ALU.add), reads=[ss], writes=[ms])
    k.op("pool", lambda e: e.tensor_tensor(out=rstd.t[0:P, :], in0=ms.t[0:P, :], in1=S.nh.t[0:P, :], op=ALU.pow),
         reads=[ms, S.nh], writes=[rstd])
    return rstd


def load_w(k, dst, src3, ncols):
    nsplit = (ncols + 2047) // 2048
    step = (ncols + nsplit - 1) // nsplit
    a = 0
    first = True
    while a < ncols:
        b = min(ncols, a + step)
        k.dma("pool", dst.t[:, :, a:b], src3[:, :, a:b], writes=[dst], join=not first)
        first = False
        a = b


def mm_group(k, out_ap, out_buf, pairs, rbufs, first_join=False):
    n = len(pairs)
    for i, (l, r) in enumerate(pairs):
        k.op("pe", lambda e: e.matmul(out_ap, lhsT=l, rhs=r, start=(i == 0), stop=(i == n - 1)),
             reads=rbufs, writes=[out_buf], inc=(i == n - 1), join=(first_join or i > 0))


def evac(k, eng, out_ap, in_ap, rbuf, wbuf, scale=None, join=False):
    if eng == "act":
        if scale is None:
            k.op("act", lambda e: e.activation(out=out_ap, in_=in_ap, func=AF.Copy), reads=[rbuf], writes=[wbuf], join=join)
        else:
            k.op("act", lambda e: e.activation(out=out_ap, in_=in_ap, func=AF.Copy, scale=scale), reads=[rbuf],
                 writes=[wbuf], join=join)
    else:
        if scale is None:
            k.op("dve", lambda e: e.tensor_copy(out=out_ap, in_=in_ap), reads=[rbuf], writes=[wbuf], join=join)
        else:
            k.op("dve", lambda e: e.tensor_scalar(out=out_ap, in0=in_ap, scalar1=scale, scalar2=None, op0=ALU.mult),
                 reads=[rbuf], writes=[wbuf], join=join)


def transposes(k, C, pT, src, P, col_offs, nslot0=0):
    n = len(col_offs)
    for i, off in enumerate(col_offs):
        k.op("pe", lambda e: e.transpose(out=pT.t[:, nslot0 + i, 0:P], in_=src.t[0:P, off:off + 128],
                                         identity=C["ident"].t[0:P, 0:P]),
             reads=[src, C["ident"]], writes=[pT], inc=(i == n - 1), join=(i > 0))


def phase_A(k, cfg, l, D, C, VS):
    k.push()
    S = Small(k, njunk=1)
    win = k.sb("win", [128, 8, 2240], BF16)
    wuq = k.sb("wuq", [128, 3, 1024], BF16)
    wukv = k.sb("wukv", [128, 2, 1024], BF16)
    wv_ = D["win"][l].rearrange("(c p) n -> p c n", p=128)
    winA = Buf("winA", win.t)
    winA.dkey = k.dsem("winA")
    k.dma("pool", win.t[:, :, 1024:2240], wv_[:, :, 1024:2240], writes=[winA])
    k.dma("pool", win.t[:, :, 0:1024], wv_[:, :, 0:1024], writes=[win])
    load_w(k, wuq, D["wuq"][l].rearrange("(c p) n -> p c n", p=128), 1024)
    load_w(k, wukv, D["wukv"][l].rearrange("(c p) n -> p c n", p=128), 1024)
    gpre = k.sb("gA", [128, 1024], F32)
    glat = k.sb("gB", [128, 640], F32)
    k.dma("sp", gpre.t[:], D["gpre"][l], writes=[gpre])
    k.dma("sp", glat.t[:], D["glat"][l], writes=[glat])
    hsrc = D["hin"] if l == 0 else D["H2"]

    hA = k.pool("sb", "h", 4, [128, 1024], F32)
    ub = k.pool("sb", "ub", 4, [128, 1024], BF16)
    uTp = k.pool("sb", "uT", 2, [128, 8, 512], BF16)
    latTp = k.pool("sb", "latT", 2, [128, 5, 512], BF16)
    latb = k.pool("sb", "latb", 2, [128, 640], BF16)
    rqp = k.pool("sb", "rq", 2, [128, 2, 512], F32)
    rkp = k.pool("sb", "rk", 2, [32, 2, 512], F32)
    ob = k.pool("sb", "ob", 8, [128, 512], BF16)
    t1p = k.pool("sb", "t1", 2, [128, 512], F32)
    t2p = k.pool("sb", "t2", 2, [128, 512], F32)
    pT = k.pool("ps", "pT", 2, [128, 8, 128], BF16)
    pm = k.pool("ps", "pm", 2, [128, 512], F32)
    pl = k.pool("ps", "pl", 2, [128, 1024], F32)
    ev = RR(["act", "dve"])

    CH = cfg["chunks"]
    cs = [dict() for _ in CH]

    def Nload(ci):
        t0, n = CH[ci]
        c = cs[ci]
        c["uT"] = uTp.next()
        c["rq"] = rqp.next()
        c["rk"] = rkp.next()
        c["us"] = {}
        k.dma("sp", c["rq"].t[:, :, 0:n], D["ropeq"][:, :, t0:t0 + n], writes=[c["rq"]])
        k.dma("sp", c["rk"].t[:, :, 0:n], D["ropek"][:, :, t0:t0 + n], writes=[c["rk"]])

    def N1load(ci, j):
        t0, n = CH[ci]
        o, P = chunk_tiles(n)[j]
        h = hA.next()
        cs[ci].setdefault("hs", {})[j] = h
        k.dma("sp", h.t[0:P, :], hsrc[t0 + o:t0 + o + P, :], writes=[h])

    def N1(ci, j):
        t0, n = CH[ci]
        o, P = chunk_tiles(n)[j]
        if j not in cs[ci].get("hs", {}):
            N1load(ci, j)
        h = cs[ci]["hs"][j]
        rstd = rms_rstd(k, S, h.t[0:P, :], P, 1024, [h])
        u = ub.next()
        cs[ci]["us"][j] = u
        k.op("dve", lambda e: e.scalar_tensor_tensor(out=u.t[0:P, :], in0=h.t[0:P, :], scalar=rstd.t[0:P, :],
                                                     in1=gpre.t[0:P, :], op0=ALU.mult, op1=ALU.mult),
             reads=[h, rstd, gpre], writes=[u])

    def N2(ci, j):
        t0, n = CH[ci]
        o, P = chunk_tiles(n)[j]
        uT_ = cs[ci]["uT"]
        p = pT.next()
        transposes(k, C, p, cs[ci]["us"][j], P, [c * 128 for c in range(8)])
        evac(k, "dve", uT_.t[:, :, o:o + P], p.t[:, :, 0:P], p, uT_, join=(j > 0))

    Nload(0)
    for j in range(len(chunk_tiles(CH[0][1]))):
        N1(0, j)
        N2(0, j)

    for ci, (t0, n) in enumerate(CH):
        tiles = chunk_tiles(n)
        kt0 = 0 if ci == 0 else 4 * (ci - 1) + 1
        nt = len(tiles)
        uT = cs[ci]["uT"]
        rq = cs[ci]["rq"]
        rk = cs[ci]["rk"]
        latT = latTp.next()
        nxt = ci + 1 if ci + 1 < len(CH) else None

        def sbqk(ct):
            p = pm.next()
            mm_group(k, p.t[:, 0:n], p,
                     [(win.t[:, c, ct * 128:(ct + 1) * 128], uT.t[:, c, 0:n]) for c in range(8)], [win, uT])
            o_ = ob.next()
            evac(k, "act", o_.t[:, 0:n], p.t[:, 0:n], p, o_, scale=(0.125 if ct < 4 else None))
            k.dma("act", D["QKsb"][ct * 128:(ct + 1) * 128, t0:t0 + n], o_.t[:, 0:n], reads=[o_])

        qs = {}
        lbs = {}

        def T1(j):
            o, P = tiles[j]
            p = pm.next()
            mm_group(k, p.t[0:P, :], p, [(uT.t[:, c, o:o + P], win.t[:, c, 1024:1536]) for c in range(8)], [winA, uT])
            kt = kt0 + j
            evac(k, "act", VS["vs"].t[0:P, kt, :], p.t[0:P, :], p, VS["vs"], join=True)
            q = pl.next()
            qs[j] = q
            mm_group(k, q.t[0:P, 0:384], q, [(uT.t[:, c, o:o + P], win.t[:, c, 1536:1920]) for c in range(8)], [winA, uT])
            mm_group(k, q.t[0:P, 512:768], q, [(uT.t[:, c, o:o + P], win.t[:, c, 1920:2176]) for c in range(8)],
                     [winA, uT], first_join=True)

        def T2(j):
            o, P = tiles[j]
            q = qs[j]
            r1 = rms_rstd(k, S, q.t[0:P, 0:384], P, 384, [q])
            r2 = rms_rstd(k, S, q.t[0:P, 512:768], P, 256, [q])
            lb = latb.next()
            lbs[j] = lb
            k.op("dve", lambda e: e.scalar_tensor_tensor(out=lb.t[0:P, 0:384], in0=q.t[0:P, 0:384], scalar=r1.t[0:P, :],
                                                         in1=glat.t[0:P, 0:384], op0=ALU.mult, op1=ALU.mult),
                 reads=[q, r1, glat], writes=[lb])
            k.op("dve", lambda e: e.scalar_tensor_tensor(out=lb.t[0:P, 384:640], in0=q.t[0:P, 512:768],
                                                         scalar=r2.t[0:P, :], in1=glat.t[0:P, 384:640], op0=ALU.mult,
                                                         op1=ALU.mult),
                 reads=[q, r2, glat], writes=[lb], join=True)

        def T3(j):
            o, P = tiles[j]
            p2 = pT.next()
            transposes(k, C, p2, lbs[j], P, [c * 128 for c in range(5)])
            evac(k, "dve", latT.t[:, :, o:o + P], p2.t[:, 0:5, 0:P], p2, latT, join=(j > 0))

        T1(0)
        if nxt is not None:
            Nload(nxt)
            for j in range(len(chunk_tiles(CH[nxt][1]))):
                N1load(nxt, j)
        for j in range(nt):
            if j + 1 < nt:
                T1(j + 1)
            T2(j)
            if j >= 1:
                T3(j - 1)
        ntn = len(chunk_tiles(CH[nxt][1])) if nxt is not None else 0
        for ct in range(4):
            sbqk(ct)
            if ct < ntn:
                N1(nxt, ct)
        T3(nt - 1)
        for ct in range(4, 8):
            sbqk(ct)
        pk = pm.next()
        mm_group(k, pk.t[0:32, 0:n], pk, [(win.t[:, c, 2176:2208], uT.t[:, c, 0:n]) for c in range(8)], [winA, uT])
        pks = pm.next()
        mm_group(k, pks.t[0:32, 0:n], pks, [(win.t[:, c, 2208:2240], uT.t[:, c, 0:n]) for c in range(8)], [winA, uT])
        t1 = t1p.next()
        t2 = t2p.next()
        k.op("dve", lambda e: e.tensor_tensor(out=t1.t[0:32, 0:n], in0=pk.t[0:32, 0:n], in1=rk.t[0:32, 0, 0:n], op=ALU.mult),
             reads=[pk, rk], writes=[t1])
        k.op("dve", lambda e: e.tensor_tensor(out=t2.t[0:32, 0:n], in0=pks.t[0:32, 0:n], in1=rk.t[0:32, 1, 0:n], op=ALU.mult),
             reads=[pks, rk], writes=[t2])
        o_ = ob.next()
        k.op("pool", lambda e: e.tensor_tensor(out=o_.t[0:32, 0:n], in0=t1.t[0:32, 0:n], in1=t2.t[0:32, 0:n], op=ALU.add),
             reads=[t1, t2], writes=[o_])
        k.dma("pool", D["KR"][:, t0:t0 + n], o_.t[0:32, 0:n], reads=[o_])
        if nxt is not None:
            for j in range(len(chunk_tiles(CH[nxt][1]))):
                N2(nxt, j)
        for ct in range(4):
            p = pm.next()
            mm_group(k, p.t[:, 0:n], p, [(wuq.t[:, c, ct * 128:(ct + 1) * 128], latT.t[:, c, 0:n]) for c in range(3)],
                     [wuq, latT])
            o_ = ob.next()
            evac(k, "act", o_.t[:, 0:n], p.t[:, 0:n], p, o_, scale=SCALE_MLA)
            k.dma("act", D["QMn"][ct * 128:(ct + 1) * 128, t0:t0 + n], o_.t[:, 0:n], reads=[o_])
        for rt in range(2):
            pr = pm.next()
            mm_group(k, pr.t[:, 0:n], pr,
                     [(wuq.t[:, c, 512 + rt * 128:512 + (rt + 1) * 128], latT.t[:, c, 0:n]) for c in range(3)], [wuq, latT])
            prs = pm.next()
            mm_group(k, prs.t[:, 0:n], prs,
                     [(wuq.t[:, c, 768 + rt * 128:768 + (rt + 1) * 128], latT.t[:, c, 0:n]) for c in range(3)], [wuq, latT])
            t1 = t1p.next()
            t2 = t2p.next()
            k.op("dve", lambda e: e.tensor_tensor(out=t1.t[:, 0:n], in0=pr.t[:, 0:n], in1=rq.t[:, 0, 0:n], op=ALU.mult),
                 reads=[pr, rq], writes=[t1])
            k.op("dve", lambda e: e.tensor_tensor(out=t2.t[:, 0:n], in0=prs.t[:, 0:n], in1=rq.t[:, 1, 0:n], op=ALU.mult),
                 reads=[prs, rq], writes=[t2])
            o_ = ob.next()
            k.op("pool", lambda e: e.tensor_tensor(out=o_.t[:, 0:n], in0=t1.t[:, 0:n], in1=t2.t[:, 0:n], op=ALU.add),
                 reads=[t1, t2], writes=[o_])
            k.dma("pool", D["QMr"][rt * 128:(rt + 1) * 128, t0:t0 + n], o_.t[:, 0:n], reads=[o_])
        for ct in range(4):
            p = pm.next()
            mm_group(k, p.t[:, 0:n], p, [(wukv.t[:, c, ct * 128:(ct + 1) * 128], latT.t[:, 3 + c, 0:n]) for c in range(2)],
                     [wukv, latT])
            o_ = ob.next()
            evac(k, "act", o_.t[:, 0:n], p.t[:, 0:n], p, o_)
            k.dma("act", D["KMn"][ct * 128:(ct + 1) * 128, t0:t0 + n], o_.t[:, 0:n], reads=[o_])
        for j, (o, P) in enumerate(tiles):
            p = pm.next()
            mm_group(k, p.t[0:P, :], p, [(latT.t[:, 3 + c, o:o + P], wukv.t[:, c, 512:1024]) for c in range(2)], [wukv, latT])
            kt = kt0 + j
            k.op("act", lambda e: e.activation(out=VS["vm65"].t[0:P, kt, :, 0:64],
                                               in_=p.t[0:P, :].rearrange("p (h d) -> p h d", h=8), func=AF.Copy),
                 reads=[p], writes=[VS["vm65"]], join=True)
    k.pop()


def chunk_geom(cfg, ci, i):
    if ci == 0:
        return 0, True
    m = i - (4 * (ci - 1) + 1)
    if m >= 0:
        return 128 * m, True
    return 0, False


def phase_B(k, cfg, l, D, C, last, VS):
    k.push()
    L = cfg["L"]
    chunks = cfg["chunks"]
    vs, vm65 = VS["vs"], VS["vm65"]
    qp = k.pool("sb", "qT", 2, [96, L], BF16)
    kp = k.pool("sb", "kT", 2, [96, L], BF16)
    ep = k.pool("sb", "e", 2, [128, 512], F32)
    spp = k.pool("sb", "sp", 3, [128, 512], BF16)
    ptp = k.pool("sb", "pt", 3, [128, 512], BF16)
    S32 = [k.sb("S32_%d" % i, [128, 512], F32) for i in range(2)]
    Sbf = [k.sb("Sbf%d" % i, [128, 512], BF16) for i in range(2)]
    osb = k.pool("sb", "osb", 2, [128, 4, 64], F32)
    lsb = k.pool("sb", "lsb", 2, [128, 4], F32)
    rlp = k.pool("sb", "rl", 2, [128, 4], F32)
    pa_p = k.pool("ps", "pa", 5, [128, 512], F32)
    po_p = k.pool("ps", "po", 2, [128, 512], F32)
    pdum = k.ps("pdum", [128, 512], F32)
    NDUM = cfg.get("ndum", 2)
    ident, ntri, nones, msb, mmla, zeros = (C[x] for x in ("ident", "ntri", "nones", "msb", "mmla", "zeros"))

    def dummies(nd):
        for _ in range(nd):
            k.op("pe", lambda e: e.matmul(pdum.t[:, :], lhsT=zeros.t[:, 0:128], rhs=zeros.t[:, 0:512], start=True, stop=True),
                 reads=[zeros], writes=[pdum], inc=False, join=True)

    def store_o(o_, ci, t0, n, col0):
        if ci == 0:
            k.dma("sp", D["Ocat"][0:N_META, col0:col0 + 64], o_.t[0:N_META, 0, :], reads=[o_])
        else:
            k.dma("sp", D["Ocat"][t0:t0 + n, col0:col0 + 64].rearrange("(j p) d -> p j d", p=128), o_.t[:, :, :], reads=[o_])

    def make_pairs(reverse):
        prs = []
        for h in range(8):
            for ci, (t0, n) in enumerate(chunks):
                if last and ci == 0:
                    continue
                kts = [0] if ci == 0 else (list(range(4 * ci, -1, -1)) if reverse else list(range(0, 4 * ci + 1)))
                for x, i in enumerate(kts):
                    ktok, ksz = ktile(i)
                    c0, diag = chunk_geom(cfg, ci, i)
                    prs.append(dict(h=h, ci=ci, t0=t0, n=n, x=x, first=(x == 0), lastk=(x == len(kts) - 1), i=i,
                                    ktok=ktok, ksz=ksz, c0=c0, diag=diag, dw=min(128, n - c0), qtiles=chunk_tiles(n)))
        return prs

    prs = make_pairs(True)
    N = len(prs)
    heads = {}
    cst = {}

    def head_tiles(h):
        if h not in heads:
            qT = qp.next()
            kT = kp.next()
            k.dma("sp", qT.t[0:64, :], D["QKsb"][h * 64:(h + 1) * 64, :], writes=[qT])
            k.dma("sp", kT.t[0:64, :], D["QKsb"][512 + h * 64:512 + (h + 1) * 64, :], writes=[kT])
            heads[h] = (qT, kT)
        return heads[h]

    def PE1(y):
        s = prs[y]
        qT, kT = head_tiles(s["h"])
        pa = pa_p.next()
        s["pa"] = pa
        ksz, c0, ktok, t0, n = s["ksz"], s["c0"], s["ktok"], s["t0"], s["n"]
        k.op("pe", lambda e: e.matmul(pa.t[0:ksz, c0:n], lhsT=kT.t[0:64, ktok:ktok + ksz], rhs=qT.t[0:64, t0 + c0:t0 + n],
                                      start=True, stop=True, skip_group_check=True),
             reads=[kT, qT], writes=[pa], inc=not s["diag"])
        if s["diag"]:
            dw = s["dw"]
            k.op("pe", lambda e: e.matmul(pa.t[0:ksz, c0:c0 + dw], lhsT=ident.t[0:ksz, 0:ksz], rhs=msb.t[0:ksz, 0:dw],
                                          start=False, stop=True, skip_group_check=True),
                 reads=[ident, msb], writes=[pa], join=True)

    def A1(y):
        s = prs[y]
        e_ = ep.next()
        s["e"] = e_
        ksz, c0, pa, n = s["ksz"], s["c0"], s["pa"], s["n"]
        k.op("act", lambda e: e.activation(out=e_.t[0:ksz, c0:n], in_=pa.t[0:ksz, c0:n], func=AF.Exp),
             reads=[pa], writes=[e_])

    def A2(y):
        s = prs[y]
        sp = spp.next()
        s["sp"] = sp
        ksz, c0, e_, n = s["ksz"], s["c0"], s["e"], s["n"]
        k.op("act", lambda e: e.activation(out=sp.t[0:ksz, c0:n], in_=e_.t[0:ksz, c0:n], func=AF.Ln, bias=1.0),
             reads=[e_], writes=[sp])

    def V(y):
        s = prs[y]
        if s["lastk"]:
            return
        c0, sp, n = s["c0"], s["sp"], s["n"]
        s32 = S32[s["ci"] % 2]
        if s["first"]:
            k.op("pool", lambda e: e.memset(s32.t[:], 0.0), writes=[s32])
        sb_ = Sbf[y % 2]
        k.op("dve", lambda e: e.tensor_tensor(out=s32.t[:, c0:n], in0=s32.t[:, c0:n], in1=sp.t[:, c0:n], op=ALU.add),
             reads=[sp, s32], writes=[s32])
        k.op("dve", lambda e: e.tensor_copy(out=sb_.t[:, c0:n], in_=s32.t[:, c0:n]), reads=[s32], writes=[sb_])

    def PE2(y):
        s = prs[y]
        ksz, c0, pa, sp, n = s["ksz"], s["c0"], s["pa"], s["sp"], s["n"]
        cS0 = (c0 + 128) if s["diag"] else 0
        useS = (s["x"] > 0) and (cS0 < n)
        k.op("pe", lambda e: e.matmul(pa.t[0:ksz, c0:n], lhsT=ntri.t[0:ksz, 0:ksz], rhs=sp.t[0:ksz, c0:n],
                                      start=False, stop=True, skip_group_check=True),
             reads=[ntri, sp], writes=[pa], inc=not useS, join=True)
        if useS:
            sb_ = Sbf[(y - 1) % 2]
            k.op("pe", lambda e: e.matmul(pa.t[0:ksz, cS0:n], lhsT=nones.t[:, 0:ksz], rhs=sb_.t[:, cS0:n],
                                          start=False, stop=True, skip_group_check=True),
                 reads=[nones, sb_], writes=[pa], join=True)

    def A3(y):
        s = prs[y]
        pt = ptp.next()
        s["pt"] = pt
        ksz, c0, pa, n = s["ksz"], s["c0"], s["pa"], s["n"]
        k.op("act", lambda e: e.activation(out=pt.t[0:ksz, c0:n], in_=pa.t[0:ksz, c0:n], func=AF.Exp),
             reads=[pa], writes=[pt])

    def PE3(y):
        s = prs[y]
        ksz, c0, pt, i, h, qtiles = s["ksz"], s["c0"], s["pt"], s["i"], s["h"], s["qtiles"]
        key = (h, s["ci"])
        if s["first"]:
            po = po_p.next()
            cst[key] = po
            k.op("pe", lambda e: e.matmul(po.t[:, 0:256], lhsT=zeros.t[:, 0:128], rhs=zeros.t[:, 0:256], start=True,
                                          stop=False, skip_group_check=True), reads=[zeros], writes=[po], inc=False)
        po = cst[key]
        jqs = [jq for jq, (qo, qw) in enumerate(qtiles) if qo >= c0]
        for jq in jqs:
            qo, qw = qtiles[jq]
            k.op("pe", lambda e: e.matmul(po.t[0:qw, jq * 64:(jq + 1) * 64], lhsT=pt.t[0:ksz, qo:qo + qw],
                                          rhs=vs.t[0:ksz, i, h * 64:(h + 1) * 64], start=False, stop=s["lastk"],
                                          skip_group_check=True),
                 reads=[pt, vs], writes=[po], inc=(jq == jqs[-1]), join=True)
        if s["lastk"]:
            Pq = qtiles[0][1]
            nt = len(qtiles)
            o_ = osb.next()
            k.op("dve", lambda e: e.tensor_copy(out=o_.t[0:Pq, 0:nt, :],
                                                in_=po.t[0:Pq, 0:nt * 64].rearrange("p (j d) -> p j d", d=64)),
                 reads=[po], writes=[o_])
            store_o(o_, s["ci"], s["t0"], s["n"], h * 64)

    PE1(0)
    A1(0)
    if N > 1:
        PE1(1)
    for y in range(N):
        A2(y)
        V(y)
        if y + 1 < N:
            A1(y + 1)
        PE2(y)
        if y + 2 < N:
            PE1(y + 2)
        if y >= 1:
            A3(y - 1)
            dummies(NDUM)
            PE3(y - 1)
    A3(N - 1)
    PE3(N - 1)

    prs = make_pairs(False)
    N = len(prs)
    heads = {}
    cst = {}

    def mla_tiles(h):
        if h not in heads:
            qT = qp.next()
            kT = kp.next()
            k.dma("sp", qT.t[0:64, :], D["QMn"][h * 64:(h + 1) * 64, :], writes=[qT])
            k.dma("sp", qT.t[64:96, :], D["QMr"][h * 32:(h + 1) * 32, :], writes=[qT], join=True)
            k.dma("sp", kT.t[0:64, :], D["KMn"][h * 64:(h + 1) * 64, :], writes=[kT])
            k.dma("sp", kT.t[64:96, :], D["KR"][:, :], writes=[kT], join=True)
            heads[h] = (qT, kT)
        return heads[h]

    def M1(y):
        s = prs[y]
        qT, kT = mla_tiles(s["h"])
        pa = pa_p.next()
        s["pa"] = pa
        ksz, c0, ktok, t0, n = s["ksz"], s["c0"], s["ktok"], s["t0"], s["n"]
        k.op("pe", lambda e: e.matmul(pa.t[0:ksz, c0:n], lhsT=kT.t[:, ktok:ktok + ksz], rhs=qT.t[:, t0 + c0:t0 + n],
                                      start=True, stop=not s["diag"]), reads=[kT, qT], writes=[pa], inc=not s["diag"])
        if s["diag"]:
            dw = s["dw"]
            k.op("pe", lambda e: e.matmul(pa.t[0:ksz, c0:c0 + dw], lhsT=ident.t[0:ksz, 0:ksz], rhs=mmla.t[0:ksz, 0:dw],
                                          start=False, stop=True), reads=[ident, mmla], writes=[pa], join=True)

    def MA(y):
        s = prs[y]
        pt = ptp.next()
        s["pt"] = pt
        ksz, c0, pa, n = s["ksz"], s["c0"], s["pa"], s["n"]
        k.op("act", lambda e: e.activation(out=pt.t[0:ksz, c0:n], in_=pa.t[0:ksz, c0:n], func=AF.Exp),
             reads=[pa], writes=[pt])

    def M3(y):
        s = prs[y]
        ksz, c0, pt, i, h, qtiles = s["ksz"], s["c0"], s["pt"], s["i"], s["h"], s["qtiles"]
        key = (h, s["ci"])
        if s["first"]:
            po = po_p.next()
            cst[key] = po
            k.op("pe", lambda e: e.matmul(po.t[:, 0:260], lhsT=zeros.t[:, 0:128], rhs=zeros.t[:, 0:260], start=True,
                                          stop=False, skip_group_check=True), reads=[zeros], writes=[po], inc=False)
        po = cst[key]
        jqs = [jq for jq, (qo, qw) in enumerate(qtiles) if qo >= c0]
        for jq in jqs:
            qo, qw = qtiles[jq]
            k.op("pe", lambda e: e.matmul(po.t[0:qw, jq * 65:jq * 65 + 65], lhsT=pt.t[0:ksz, qo:qo + qw],
                                          rhs=vm65.t[0:ksz, i, h, :], start=False, stop=s["lastk"],
                                          skip_group_check=True),
                 reads=[pt, vm65], writes=[po], inc=(jq == jqs[-1]), join=True)
        if s["lastk"]:
            Pq = qtiles[0][1]
            nt = len(qtiles)
            ls = lsb.next()
            rl = rlp.next()
            o_ = osb.next()
            pov = po.t[0:Pq, 0:nt * 65].rearrange("p (j c) -> p j c", c=65)
            k.op("dve", lambda e: e.tensor_copy(out=ls.t[0:Pq, 0:nt], in_=pov[:, :, 64]), reads=[po], writes=[ls])
            k.op("dve", lambda e: e.reciprocal(out=rl.t[0:Pq, 0:nt], in_=ls.t[0:Pq, 0:nt]), reads=[ls], writes=[rl])
            for jq in range(nt):
                k.op("dve", lambda e: e.tensor_scalar(out=o_.t[0:Pq, jq, :], in0=po.t[0:Pq, jq * 65:jq * 65 + 64],
                                                      scalar1=rl.t[0:Pq, jq:jq + 1], scalar2=None, op0=ALU.mult),
                     reads=[po, rl], writes=[o_], join=(jq > 0))
            store_o(o_, s["ci"], s["t0"], s["n"], 512 + h * 64)

    M1(0)
    if N > 1:
        M1(1)
    for y in range(N):
        MA(y)
        if y + 2 < N:
            M1(y + 2)
        M3(y)
    k.pop()


def phase_C(k, cfg, l, D, C, last, W=None):
    k.push()
    S = Small(k)
    wo = k.sb("wo", [128, 8, 1024], BF16)
    load_w(k, wo, D["wo"][l].rearrange("(c p) n -> p c n", p=128), 1024)
    gout = k.sb("gA", [128, 1024], F32)
    gpost = k.sb("gB", [128, 1024], F32)
    k.dma("sp", gout.t[:], D["gout"][l], writes=[gout])
    k.dma("sp", gpost.t[:], D["gpost"][l], writes=[gpost])
    hsrc = D["hin"] if l == 0 else D["H2"]
    ocp = k.pool("sb", "oc", 2, [128, 1024], F32)
    hp = k.pool("sb", "h", 3, [128, 1024], F32)
    mbp = k.pool("sb", "mb", 2, [128, 1024], BF16)
    mTp = k.pool("sb", "mT", 2, [128, 8, 128], BF16)
    ttp = k.pool("sb", "tt", 2, [128, 1024], F32)
    pT = k.pool("ps", "pT", 2, [128, 8, 128], BF16)
    pmix = k.pool("ps", "pmix", 2, [128, 1024], F32)
    ev = RR(["act", "dve"])
    tl = []
    for ci, (t0c, n) in enumerate(cfg["chunks"]):
        if last and ci == 0:
            continue
        for (o, P) in chunk_tiles(n):
            tl.append((t0c + o, P))
    st = [dict() for _ in tl]

    def S1(j):
        t0, P = tl[j]
        oc = ocp.next()
        h = hp.next()
        st[j]["h"] = h
        k.dma("sp", oc.t[0:P, :], D["Ocat"][t0:t0 + P, :], writes=[oc])
        k.dma("sp", h.t[0:P, :], hsrc[t0:t0 + P, :], writes=[h])
        r1 = rms_rstd(k, S, oc.t[0:P, 0:512], P, 512, [oc])
        r2 = rms_rstd(k, S, oc.t[0:P, 512:1024], P, 512, [oc])
        mb = mbp.next()
        st[j]["mb"] = mb
        k.op("dve", lambda e: e.scalar_tensor_tensor(out=mb.t[0:P, 0:512], in0=oc.t[0:P, 0:512], scalar=r1.t[0:P, :],
                                                     in1=gout.t[0:P, 0:512], op0=ALU.mult, op1=ALU.mult),
             reads=[oc, r1, gout], writes=[mb])
        k.op("dve", lambda e: e.scalar_tensor_tensor(out=mb.t[0:P, 512:1024], in0=oc.t[0:P, 512:1024], scalar=r2.t[0:P, :],
                                                     in1=gout.t[0:P, 512:1024], op0=ALU.mult, op1=ALU.mult),
             reads=[oc, r2, gout], writes=[mb], join=True)

    def S2(j):
        t0, P = tl[j]
        mb = st[j]["mb"]
        p = pT.next()
        transposes(k, C, p, mb, P, [c * 128 for c in range(8)])
        mT = mTp.next()
        evac(k, "act", mT.t[:, :, 0:P], p.t[:, :, 0:P], p, mT)
        pmx = pmix.next()
        st[j]["pmx"] = pmx
        mm_group(k, pmx.t[0:P, 0:512], pmx, [(mT.t[:, c, 0:P], wo.t[:, c, 0:512]) for c in range(8)], [mT, wo])
        mm_group(k, pmx.t[0:P, 512:1024], pmx, [(mT.t[:, c, 0:P], wo.t[:, c, 512:1024]) for c in range(8)], [mT, wo],
                 first_join=True)

    def S3(j):
        t0, P = tl[j]
        pmx, h = st[j]["pmx"], st[j]["h"]
        r3 = rms_rstd(k, S, pmx.t[0:P, :], P, 1024, [pmx])
        tt = ttp.next()
        k.op("dve", lambda e: e.scalar_tensor_tensor(out=tt.t[0:P, :], in0=pmx.t[0:P, :], scalar=r3.t[0:P, :],
                                                     in1=gpost.t[0:P, :], op0=ALU.mult, op1=ALU.mult),
             reads=[pmx, r3, gpost], writes=[tt])
        k.op("dve", lambda e: e.tensor_tensor(out=tt.t[0:P, :], in0=tt.t[0:P, :], in1=h.t[0:P, :], op=ALU.add),
             reads=[tt, h], writes=[tt])
        k.dma("pool", D["H1"][t0:t0 + P, :], tt.t[0:P, :], reads=[tt])

    nT = len(tl)
    for j in range(nT + 2):
        if j == nT // 4:
            ffn_load(k, W, l, D)
        if j < nT:
            S1(j)
        if 0 <= j - 1 < nT:
            S2(j - 1)
        if 0 <= j - 2 < nT:
            S3(j - 2)
    k.pop()


def ffn_weights(k, cfg, l, D):
    NF = D_FF // 128
    W = {}
    W["wg"] = k.sb("wg", [128, 8, D_FF], BF16)
    W["wu"] = k.sb("wu", [128, 8, D_FF], BF16)
    W["wd"] = k.sb("wd", [128, NF, 1024], BF16)
    W["gpf"] = k.sb("gC", [128, 1024], F32)
    W["gpo"] = k.sb("gD", [128, 1024], F32)
    W["loaded"] = False
    return W


def ffn_load(k, W, l, D):
    if W is None or W["loaded"]:
        return
    W["loaded"] = True
    load_w(k, W["wg"], D["wg"][l].rearrange("(c p) n -> p c n", p=128), D_FF)
    load_w(k, W["wu"], D["wu"][l].rearrange("(c p) n -> p c n", p=128), D_FF)
    k.dma("sp", W["gpf"].t[:], D["gpf"][l], writes=[W["gpf"]])
    k.dma("sp", W["gpo"].t[:], D["gpo"][l], writes=[W["gpo"]])


def phase_D(k, cfg, l, D, C, last, W):
    k.push()
    load_w(k, W["wd"], D["wd"][l].rearrange("(c p) n -> p c n", p=128), 1024)
    S = Small(k, njunk=1)
    NF = D_FF // 128
    wg, wu, wd, gpf, gpo = W["wg"], W["wu"], W["wd"], W["gpf"], W["gpo"]
    xp = k.pool("sb", "x", 4, [128, 1024], F32)
    fbp = k.pool("sb", "ub", 2, [128, 1024], BF16)
    fTp = k.pool("sb", "fT", 2, [128, 8, 512], BF16)
    hT = k.sb("hT", [128, NF, 512], BF16)
    sgp = k.pool("sb", "sg", 1, [128, 512], F32)
    pT = k.pool("ps", "pT", 2, [128, 8, 128], BF16)
    pp = k.pool("ps", "pp", 6, [128, 512], F32)
    ev = RR(["act", "dve"])
    chs = [(ci, t0, n) for ci, (t0, n) in enumerate(cfg["chunks"]) if not (last and ci == 0)]
    fTs = {}

    def D1(x):
        ci, t0, n = chs[x]
        tiles = chunk_tiles(n)
        fT = fTp.next()
        fTs[x] = fT
        fbs = {}

        def N1(j):
            o, P = tiles[j]
            h = xp.next()
            k.dma("sp", h.t[0:P, :], D["H1"][t0 + o:t0 + o + P, :], writes=[h])
            rstd = rms_rstd(k, S, h.t[0:P, :], P, 1024, [h])
            fb = fbp.next()
            fbs[j] = fb
            k.op("dve", lambda e: e.scalar_tensor_tensor(out=fb.t[0:P, :], in0=h.t[0:P, :], scalar=rstd.t[0:P, :],
                                                         in1=gpf.t[0:P, :], op0=ALU.mult, op1=ALU.mult),
                 reads=[h, rstd, gpf], writes=[fb])

        def N2(j):
            o, P = tiles[j]
            p = pT.next()
            transposes(k, C, p, fbs[j], P, [c * 128 for c in range(8)])
            evac(k, ev.next(), fT.t[:, :, o:o + P], p.t[:, :, 0:P], p, fT, join=(j > 0))

        steps = [lambda: N1(0)]
        for j in range(len(tiles)):
            def step(j=j):
                if j + 1 < len(tiles):
                    N1(j + 1)
                N2(j)
            steps.append(step)
        return steps

    def D2(x, extra):
        ci, t0, n = chs[x]
        fT = fTs[x]
        for f in range(NF):
            if f >= 6 and f % 3 == 0 and extra:
                extra.pop(0)()
            g_ = pp.next()
            u_ = pp.next()
            mm_group(k, g_.t[:, 0:n], g_, [(wg.t[:, c, f * 128:(f + 1) * 128], fT.t[:, c, 0:n]) for c in range(8)], [wg, fT])
            mm_group(k, u_.t[:, 0:n], u_, [(wu.t[:, c, f * 128:(f + 1) * 128], fT.t[:, c, 0:n]) for c in range(8)], [wu, fT])
            sg = sgp.next()
            k.op("act", lambda e: e.activation(out=sg.t[:, 0:n], in_=g_.t[:, 0:n], func=AF.Silu), reads=[g_], writes=[sg])
            k.op("dve", lambda e: e.tensor_tensor(out=hT.t[:, f, 0:n], in0=u_.t[:, 0:n], in1=sg.t[:, 0:n], op=ALU.mult),
                 reads=[u_, sg], writes=[hT], join=(f > 0))

    def D3(x):
        ci, t0, n = chs[x]
        tiles = chunk_tiles(n)
        for j, (o, P) in enumerate(tiles):
            dA = pp.next()
            dB = pp.next()
            mm_group(k, dA.t[0:P, :], dA, [(hT.t[:, f, o:o + P], wd.t[:, f, 0:512]) for f in range(NF)], [hT, wd])
            mm_group(k, dB.t[0:P, :], dB, [(hT.t[:, f, o:o + P], wd.t[:, f, 512:1024]) for f in range(NF)], [hT, wd])
            hr = xp.next()
            k.dma("sp", hr.t[0:P, :], D["H1"][t0 + o:t0 + o + P, :], writes=[hr])
            tt = xp.next()
            k.op("act", lambda e: e.activation(out=tt.t[0:P, 0:512], in_=dA.t[0:P, :], func=AF.Copy), reads=[dA], writes=[tt])
            k.op("act", lambda e: e.activation(out=tt.t[0:P, 512:1024], in_=dB.t[0:P, :], func=AF.Copy), reads=[dB],
                 writes=[tt], join=True)
            r3 = rms_rstd(k, S, tt.t[0:P, :], P, 1024, [tt])
            k.op("dve", lambda e: e.scalar_tensor_tensor(out=tt.t[0:P, :], in0=tt.t[0:P, :], scalar=r3.t[0:P, :],
                                                         in1=gpo.t[0:P, :], op0=ALU.mult, op1=ALU.mult),
                 reads=[tt, r3, gpo], writes=[tt])
            k.op("pool", lambda e: e.tensor_tensor(out=tt.t[0:P, :], in0=tt.t[0:P, :], in1=hr.t[0:P, :], op=ALU.add),
                 reads=[tt, hr], writes=[tt])
            if last:
                k.dma("pool", D["out"][t0 + o - N_META:t0 + o - N_META + P, :], tt.t[0:P, :], reads=[tt])
            else:
                k.dma("pool", D["H2"][t0 + o:t0 + o + P, :], tt.t[0:P, :], reads=[tt])

    for st_ in D1(0):
        st_()
    for x in range(len(chs)):
        extra = D1(x + 1) if x + 1 < len(chs) else []
        D2(x, extra)
        for st_ in extra:
            st_()
        D3(x)
    k.pop()


IN_SPECS = None


def in_specs(cfg):
    NL, L = cfg["nl"], cfg["L"]
    return [
        ("hin", [L, 1024], F32), ("win", [NL, 1024, 2240], F32), ("wuq", [NL, Q_LORA, 1024], F32),
        ("wukv", [NL, KV_LORA, 1024], F32), ("wo", [NL, 1024, 1024], F32), ("wg", [NL, 1024, D_FF], F32),
        ("wu", [NL, 1024, D_FF], F32), ("wd", [NL, D_FF, 1024], F32),
        ("gpre", [NL, 128, 1024], F32), ("glat", [NL, 128, 640], F32), ("gout", [NL, 128, 1024], F32),
        ("gpost", [NL, 128, 1024], F32), ("gpf", [NL, 128, 1024], F32), ("gpo", [NL, 128, 1024], F32),
        ("ropeq", [128, 2, L], F32), ("ropek", [32, 2, L], F32),
        ("ident", [128, 128], BF16), ("ntri", [128, 128], BF16), ("nones", [128, 128], BF16),
        ("msb", [128, 128], BF16), ("mmla", [128, 128], BF16), ("zeros", [128, 512], BF16), ("ones", [128, 8], BF16),
    ]


def scratch_specs(cfg):
    L = cfg["L"]
    return [
        ("QKsb", [1024, L], BF16), ("Vsb", [L, 512], BF16), ("QMn", [512, L], BF16), ("QMr", [256, L], BF16),
        ("KMn", [512, L], BF16), ("KR", [32, L], BF16), ("Vm", [L, 512], BF16), ("Ocat", [L, 1024], F32),
        ("H1", [L, 1024], F32), ("H2", [L, 1024], F32),
    ]


def build_program(cfg, phases=None):
    nc = bass.Bass("TRN2", target_bir_lowering=False)
    D = {}
    for name, shape, dt in in_specs(cfg):
        D[name] = nc.dram_tensor(name, shape, dt, kind="ExternalInput").ap()
    for name, shape, dt in scratch_specs(cfg):
        D[name] = nc.dram_tensor(name, shape, dt, kind=("ExternalOutput" if cfg["debug"] else "Internal")).ap()
    D["out"] = nc.dram_tensor("out", [cfg["L"] - N_META, 1024], F32, kind="ExternalOutput").ap()
    with ExitStack() as es:
        k = K(nc, es)
        C = {}
        for name, shape, dt in in_specs(cfg):
            if name in ("ident", "ntri", "nones", "msb", "mmla", "zeros", "ones"):
                C[name] = k.sb("c_" + name, shape, dt, dkey="consts")
                k.dma("sp", C[name].t[:], D[name], writes=[C[name]])
        for l in range(cfg["nl"]):
            last = (l == cfg["nl"] - 1) and not cfg["debug"]
            k.push()
            VS = dict(vs=k.sb("vs", [128, cfg["nkt"], 512], BF16), vm65=k.sb("vm65", [128, cfg["nkt"], 8, 65], BF16))
            k.op("pool", lambda e: e.memset(VS["vm65"].t[:, :, :, 64:65], 1.0), writes=[VS["vm65"]])
            k.op("pool", lambda e: e.memset(VS["vs"].t[:, 0, :], 0.0), writes=[VS["vs"]])
            if phases is None or "A" in phases:
                phase_A(k, cfg, l, D, C, VS)
            if phases is None or "B" in phases:
                phase_B(k, cfg, l, D, C, last, VS)
            k.pop()
            k.push()
            W = ffn_weights(k, cfg, l, D) if (phases is None or "D" in phases) else None
            if phases is None or "C" in phases:
                phase_C(k, cfg, l, D, C, last, W)
            if phases is None or "D" in phases:
                ffn_load(k, W, l, D)
                phase_D(k, cfg, l, D, C, (l == cfg["nl"] - 1), W)
            k.pop()
        k.finish()
        build_program.stats = (k.nins, k.nwaits)
    return nc


def host_shared(cfg, inp):
    NL, L = cfg["nl"], cfg["L"]
    f32 = np.float32
    bf = ml_dtypes.bfloat16
    o1, o2, o3 = 3 * SBW, 3 * SBW + Q_LORA, 3 * SBW + Q_LORA + KV_LORA
    w_in = np.asarray(inp["w_in"], f32)[:NL]
    win = np.concatenate([w_in, w_in[:, :, o3 + 16:o3 + 32], w_in[:, :, o3:o3 + 16]], axis=2)
    w_uq = np.asarray(inp["w_uq"], f32)[:NL].reshape(NL, Q_LORA, 8, 96)
    nope = w_uq[..., :64].reshape(NL, Q_LORA, 512)
    rope = w_uq[..., 64:].reshape(NL, Q_LORA, 256)
    ropes = np.concatenate([w_uq[..., 80:96], w_uq[..., 64:80]], axis=-1).reshape(NL, Q_LORA, 256)
    wuq = np.concatenate([nope, rope, ropes], axis=2)
    w_ukv = np.asarray(inp["w_ukv"], f32)[:NL].reshape(NL, KV_LORA, 8, 128)
    wukv = np.concatenate([w_ukv[..., :64].reshape(NL, KV_LORA, 512), w_ukv[..., 64:].reshape(NL, KV_LORA, 512)], axis=2)

    def rep(g):
        g = np.asarray(g, f32)[:NL]
        return np.ascontiguousarray(np.broadcast_to(g[:, None, :], (NL, 128, g.shape[-1])))

    inv_freq = (1.0 / (np.float32(10000.0) ** (np.arange(0, ROPE, 2, dtype=f32) / np.float32(ROPE)))).astype(f32)
    ang = (np.arange(L, dtype=f32)[:, None] * inv_freq[None, :]).astype(f32)
    cos = np.cos(ang).astype(f32).T
    sin = np.sin(ang).astype(f32).T
    c2 = np.concatenate([cos, cos], axis=0)
    s2 = np.concatenate([-sin, sin], axis=0)
    ropek = np.ascontiguousarray(np.stack([c2, s2], axis=1)).astype(f32)
    ropeq = np.ascontiguousarray(np.tile(ropek, (4, 1, 1)) * np.float32(SCALE_MLA)).astype(f32)
    j = np.arange(128)[:, None]
    s = np.arange(128)[None, :]
    sh = dict(
        win=np.ascontiguousarray(win), wuq=np.ascontiguousarray(wuq), wukv=np.ascontiguousarray(wukv),
        wo=np.ascontiguousarray(np.asarray(inp["w_o"], f32)[:NL]), wg=np.ascontiguousarray(np.asarray(inp["w_gate"], f32)[:NL]),
        wu=np.ascontiguousarray(np.asarray(inp["w_up"], f32)[:NL]), wd=np.ascontiguousarray(np.asarray(inp["w_down"], f32)[:NL]),
        gpre=rep(inp["pre_mix_norm"]),
        glat=rep(np.concatenate([np.asarray(inp["q_lat_norm"], f32), np.asarray(inp["kv_lat_norm"], f32)], axis=1)),
        gout=rep(np.concatenate([np.asarray(inp["sb_out_norm"], f32), np.asarray(inp["mla_out_norm"], f32)], axis=1)),
        gpost=rep(inp["post_mix_norm"]), gpf=rep(inp["pre_ffn_norm"]), gpo=rep(inp["post_ffn_norm"]),
        ropeq=ropeq, ropek=ropek,
        ident=np.eye(128, dtype=f32).astype(bf),
        ntri=np.where(j >= s, -1.0, 0.0).astype(f32).astype(bf),
        nones=np.full((128, 128), -1.0, f32).astype(bf),
        msb=np.where(j < s, 0.0, NEG).astype(f32).astype(bf),
        mmla=np.where(j <= s, 0.0, NEG).astype(f32).astype(bf),
        zeros=np.zeros((128, 512), f32).astype(bf),
        ones=np.ones((128, 8), f32).astype(bf),
    )
    return sh


_CACHE = {}


def kernel(**inputs):
    cfg = make_cfg()
    x = np.asarray(inputs["x"], np.float32)
    B = x.shape[0]
    meta = np.asarray(inputs["meta_tokens"], np.float32)
    sh = host_shared(cfg, inputs)
    in_maps = []
    for b in range(B):
        m = dict(sh)
        m["hin"] = np.ascontiguousarray(np.concatenate([meta, x[b]], axis=0))
        in_maps.append(m)
    if "nc" not in _CACHE:
        _CACHE["nc"] = build_program(cfg)
    res = run_bass_kernel_spmd(_CACHE["nc"], in_maps, core_ids=list(range(B)))
    return np.stack([np.asarray(r["out"], np.float32) for r in res.results], axis=0)
```
